# Optimizing a Trainium2 kernel written in Bass

```python
import math
import jax, jax.numpy as jnp
from jax import lax
import numpy as np

D_MODEL = 2048
BATCH = 8
SEQ = 4096
DEPTH = 1
DEC_BATCH = 2
DEC_SEQ = 4096
PAST_LEN = 128

MLA_HEADS = 8
MLA_NOPE = 128
MLA_ROPE = 64
MLA_V = 128
Q_LORA = 512
KV_LORA = 256
GQA_HEADS = 8
GQA_KV_HEADS = 2
GQA_HEAD_DIM = 128
MEM_HEADS = 4
MEM_HEAD_DIM = 256
MEM_TOKENS = 256
BRANCH_W = 1024
N_BRANCH = 3
GRID_W = 64
BLOCK_Q = 128
ROPE_BASE = 10000.0
EPS = 1e-6

kernel_name = "hybrid_mla_gqa_mem_gated_encoder"


def _in_sizes():
    return [Q_LORA, KV_LORA, MLA_ROPE,
            GQA_HEADS * GQA_HEAD_DIM, GQA_KV_HEADS * GQA_HEAD_DIM, GQA_KV_HEADS * GQA_HEAD_DIM,
            MEM_HEADS * MEM_HEAD_DIM,
            N_BRANCH * BRANCH_W,
            N_BRANCH * D_MODEL]


def rmsnorm(x, g):
    xf = x.astype(jnp.float32)
    xf = xf * lax.rsqrt(jnp.mean(xf * xf, axis=-1, keepdims=True) + EPS)
    return xf.astype(x.dtype) * g


def grid_positions(s):
    rows = s // GRID_W
    row = jnp.repeat(jnp.arange(rows, dtype=jnp.float32), GRID_W)
    col = jnp.tile(jnp.arange(GRID_W, dtype=jnp.float32), rows)
    return row, col


def rope_1d(x, pos):
    half = x.shape[-1] // 2
    freqs = ROPE_BASE ** (-jnp.arange(half, dtype=jnp.float32) / half)
    ang = pos[:, None] * freqs[None, :]
    cos = jnp.cos(ang)[:, None, :].astype(x.dtype)
    sin = jnp.sin(ang)[:, None, :].astype(x.dtype)
    x1, x2 = x[..., :half], x[..., half:]
    return jnp.concatenate([x1 * cos - x2 * sin, x2 * cos + x1 * sin], axis=-1)


def axial_rope(x, row, col):
    d = x.shape[-1] // 2
    return jnp.concatenate([rope_1d(x[..., :d], row), rope_1d(x[..., d:], col)], axis=-1)


def block_attention(q, k, v, scale):
    b, s, g, r, dk = q.shape
    nb = s // BLOCK_Q
    qb = q.reshape(b, nb, BLOCK_Q, g, r, dk).transpose(1, 0, 2, 3, 4, 5)

    def one(qblk):
        sc = jnp.einsum('bqgrd,bkgd->bgrqk', qblk, k).astype(jnp.float32) * scale
        p = jax.nn.softmax(sc, axis=-1).astype(v.dtype)
        return jnp.einsum('bgrqk,bkgv->bqgrv', p, v)

    out = lax.map(one, qb)
    dv = v.shape[-1]
    return out.transpose(1, 0, 2, 3, 4, 5).reshape(b, s, g * r, dv)


def layer(x, mem, row, col, norm_g, w_in, mla_q_norm_g, w_q_b, mla_kv_norm_g, w_kv_b,
          gqa_q_norm_g, gqa_k_norm_g, mem_norm_g, w_mem_kv, w_branch, w_out):
    b, s, _ = x.shape
    h = rmsnorm(x, norm_g)
    proj = h @ w_in
    idx = np.cumsum(_in_sizes())[:-1].tolist()
    cq, ckv, kpe, qg, kg, vg, qm, z, gl = jnp.split(proj, idx, axis=-1)

    cq = rmsnorm(cq, mla_q_norm_g)
    q = (cq @ w_q_b).reshape(b, s, MLA_HEADS, MLA_NOPE + MLA_ROPE)
    q_nope, q_pe = q[..., :MLA_NOPE], q[..., MLA_NOPE:]
    q_pe = axial_rope(q_pe, row, col)
    ckv = rmsnorm(ckv, mla_kv_norm_g)
    kv = (ckv @ w_kv_b).reshape(b, s, MLA_HEADS, MLA_NOPE + MLA_V)
    k_nope, v_mla = kv[..., :MLA_NOPE], kv[..., MLA_NOPE:]
    k_pe = axial_rope(kpe[:, :, None, :], row, col)
    k_pe = jnp.broadcast_to(k_pe, (b, s, MLA_HEADS, MLA_ROPE))
    q_full = jnp.concatenate([q_nope, q_pe], axis=-1)[:, :, :, None, :]
    k_full = jnp.concatenate([k_nope, k_pe], axis=-1)
    o_mla = block_attention(q_full, k_full, v_mla, 1.0 / math.sqrt(MLA_NOPE + MLA_ROPE))
    o_mla = o_mla.reshape(b, s, BRANCH_W)

    qg = rmsnorm(qg.reshape(b, s, GQA_HEADS, GQA_HEAD_DIM), gqa_q_norm_g)
    kg = rmsnorm(kg.reshape(b, s, GQA_KV_HEADS, GQA_HEAD_DIM), gqa_k_norm_g)
    vg = vg.reshape(b, s, GQA_KV_HEADS, GQA_HEAD_DIM)
    qg = axial_rope(qg, row, col).reshape(b, s, GQA_KV_HEADS, GQA_HEADS // GQA_KV_HEADS, GQA_HEAD_DIM)
    kg = axial_rope(kg, row, col)
    o_gqa = block_attention(qg, kg, vg, 1.0 / math.sqrt(GQA_HEAD_DIM)).reshape(b, s, BRANCH_W)

    m = rmsnorm(mem, mem_norm_g)
    mkv = (m @ w_mem_kv).reshape(b, MEM_TOKENS, 2, MEM_HEADS, MEM_HEAD_DIM)
    km, vm = mkv[:, :, 0], mkv[:, :, 1]
    qm = qm.reshape(b, s, MEM_HEADS, MEM_HEAD_DIM)
    sc = jnp.einsum('bshd,bmhd->bhsm', qm, km).astype(jnp.float32) * (1.0 / math.sqrt(MEM_HEAD_DIM))
    pm = jax.nn.softmax(sc, axis=-1).astype(vm.dtype)
    o_mem = jnp.einsum('bhsm,bmhd->bshd', pm, vm).reshape(b, s, BRANCH_W)

    z = z.reshape(b, s, N_BRANCH, BRANCH_W)
    gl = gl.reshape(b, s, N_BRANCH, D_MODEL)
    merged = None
    for i, o in enumerate((o_mla, o_gqa, o_mem)):
        term = jax.nn.sigmoid(gl[:, :, i]) * ((o * jax.nn.silu(z[:, :, i])) @ w_branch[i])
        merged = term if merged is None else merged + term
    return x + merged @ w_out


def trunk(x, mem, norm_g, w_in, mla_q_norm_g, w_q_b, mla_kv_norm_g, w_kv_b,
          gqa_q_norm_g, gqa_k_norm_g, mem_norm_g, w_mem_kv, w_branch, w_out, final_norm_g):
    row, col = grid_positions(x.shape[1])
    h = x
    for l in range(DEPTH):
        h = layer(h, mem, row, col, norm_g[l], w_in[l], mla_q_norm_g[l], w_q_b[l], mla_kv_norm_g[l],
                  w_kv_b[l], gqa_q_norm_g[l], gqa_k_norm_g[l], mem_norm_g[l], w_mem_kv[l],
                  w_branch[l], w_out[l])
    return rmsnorm(h, final_norm_g)


def setup_inputs(seed: int = 0) -> dict:
    key = jax.random.key(seed)
    ks = jax.random.split(key, 20)
    f32 = jnp.float32
    in_cols = int(sum(_in_sizes()))

    def w(k, shape, fan_in):
        return jax.random.normal(k, shape, f32) * (fan_in ** -0.5)

    def gain(k, shape):
        return 1.0 + 0.02 * jax.random.normal(k, shape, f32)

    return {
        "x_prompt": jax.random.normal(ks[0], (BATCH, SEQ, D_MODEL), f32),
        "x_sample": jax.random.normal(ks[1], (DEC_BATCH, DEC_SEQ, D_MODEL), f32),
        "mem_prompt": jax.random.normal(ks[2], (BATCH, MEM_TOKENS, D_MODEL), f32),
        "mem_sample": jax.random.normal(ks[3], (DEC_BATCH, MEM_TOKENS, D_MODEL), f32),
        "norm_g": gain(ks[4], (DEPTH, D_MODEL)),
        "w_in": w(ks[5], (DEPTH, D_MODEL, in_cols), D_MODEL),
        "mla_q_norm_g": gain(ks[6], (DEPTH, Q_LORA)),
        "w_q_b": w(ks[7], (DEPTH, Q_LORA, MLA_HEADS * (MLA_NOPE + MLA_ROPE)), Q_LORA),
        "mla_kv_norm_g": gain(ks[8], (DEPTH, KV_LORA)),
        "w_kv_b": w(ks[9], (DEPTH, KV_LORA, MLA_HEADS * (MLA_NOPE + MLA_V)), KV_LORA),
        "gqa_q_norm_g": gain(ks[10], (DEPTH, GQA_HEAD_DIM)),
        "gqa_k_norm_g": gain(ks[11], (DEPTH, GQA_HEAD_DIM)),
        "mem_norm_g": gain(ks[12], (DEPTH, D_MODEL)),
        "w_mem_kv": w(ks[13], (DEPTH, D_MODEL, 2 * MEM_HEADS * MEM_HEAD_DIM), D_MODEL),
        "w_branch": w(ks[14], (DEPTH, N_BRANCH, BRANCH_W, D_MODEL), BRANCH_W),
        "w_out": w(ks[15], (DEPTH, D_MODEL, D_MODEL), D_MODEL),
        "final_norm_g": gain(ks[16], (D_MODEL,)),
    }


def reference(x_prompt, x_sample, mem_prompt, mem_sample, norm_g, w_in, mla_q_norm_g, w_q_b,
              mla_kv_norm_g, w_kv_b, gqa_q_norm_g, gqa_k_norm_g, mem_norm_g, w_mem_kv,
              w_branch, w_out, final_norm_g):
    y_prompt = trunk(x_prompt, mem_prompt, norm_g, w_in, mla_q_norm_g, w_q_b, mla_kv_norm_g, w_kv_b,
                     gqa_q_norm_g, gqa_k_norm_g, mem_norm_g, w_mem_kv, w_branch, w_out, final_norm_g)
    y_sample = trunk(x_sample, mem_sample, norm_g, w_in, mla_q_norm_g, w_q_b, mla_kv_norm_g, w_kv_b,
                     gqa_q_norm_g, gqa_k_norm_g, mem_norm_g, w_mem_kv, w_branch, w_out, final_norm_g)
    return (y_prompt, y_sample)
```

```python
import contextlib
import numpy as np
import concourse.bass as bass
import concourse.mybir as mybir
from concourse.bass_utils import run_bass_kernel_spmd

F32 = mybir.dt.float32
BF16 = mybir.dt.bfloat16
AF = mybir.ActivationFunctionType
ALU = mybir.AluOpType

ENGS = ("pe", "act", "dve", "pool", "sp")
EPS = 1e-6
NBUF = 12
SEQ = 4096
D = 2048

S_CKV, S_KG, S_KPE, S_VG, S_WKVB_K, S_WKVB_V = 0, 2, 4, 5, 7, 8
S_WMK, S_WMV = 9, 17
S_CQ, S_WQB = 25, 29
S_Z = [33, 73, 113]
S_WB = [41, 81, 121]
S_GL = [49, 89, 129]
S_QG, S_QM = 65, 105
S_WO = 145
NS = 161
CONV_CH = 8


class Ev:
    __slots__ = ("sem", "val")

    def __init__(self, sem, val):
        self.sem = sem
        self.val = val


class _Dummy:
    def __getitem__(self, idx):
        return self

    def __getattr__(self, name):
        return self

    def __call__(self, *a, **k):
        return self


_DUMMY = _Dummy()


def I(name, *args, **kwargs):
    return lambda eng: getattr(eng, name)(*args, **kwargs)


class T:
    def __init__(self, t, name=""):
        self.t = t
        self.name = name
        self.w = []
        self.rs = []
        self.dsem = None
        self.al = []
        self.pw_open = False
        self.psum = False
        self.pre = []
        self.pre_new = []

    def __getitem__(self, idx):
        if self.t is None:
            return _DUMMY
        return self.t[idx]


class DSem:
    def __init__(self, sem):
        self.sem = sem
        self.cnt = 0


def _compact(evs):
    best = {}
    for ev in evs:
        k = id(ev.sem)
        if k not in best or best[k].val < ev.val:
            best[k] = ev
    return list(best.values())


class B:
    def __init__(self, nc, dry=False):
        self.nc = nc
        self.dry = dry
        self.ops = {e: [] for e in ENGS}
        self.es = contextlib.ExitStack()
        self.sem = {}
        self.cnt = {e: 0 for e in ENGS}
        self.known = {e: {} for e in ENGS}
        self.final_evs = []
        self.nsem = 0
        self.maxops = 10 ** 9
        self.nops = 0
        if not dry:
            for e in ("pe", "act", "dve", "pool"):
                self.sem[e] = self.new_sem("p_" + e)

    def new_sem(self, name):
        self.nsem += 1
        return self.es.enter_context(self.nc.semaphore(name))

    def sb(self, name, shape, dtype):
        if self.dry:
            return T(None, name)
        return T(self.es.enter_context(self.nc.sbuf_tensor("s_" + name, list(shape), dtype)), name)

    def ps(self, name, shape, dtype):
        if self.dry:
            return T(None, name)
        t = T(self.es.enter_context(self.nc.psum_tensor("p_" + name, list(shape), dtype)), name)
        t.psum = True
        return t

    def dram(self, name, shape, dtype, kind=None):
        if self.dry:
            return T(None, name)
        if kind is None:
            t = self.nc.dram_tensor(name, list(shape), dtype)
        else:
            t = self.nc.dram_tensor(name, list(shape), dtype, kind=kind)
        return T(t.ap(), name)

    def _wait(self, E, ev):
        if E == "pe" and ev.sem is self.sem.get("pe"):
            return
        k = self.known[E]
        sid = id(ev.sem)
        if k.get(sid, 0) >= ev.val:
            return
        k[sid] = ev.val
        sem, val = ev.sem, ev.val
        self.ops[E].append(lambda eng: eng.wait_ge(sem, val))

    def _deps(self, E, reads, writes, pwrites=()):
        for r in reads:
            for ev in r.w:
                self._wait(E, ev)
            if r.psum:
                own = self.sem.get(E)
                for ev in r.rs:
                    if ev.sem is not own:
                        self._wait(E, ev)
            for a in r.al:
                for ev in a.w:
                    self._wait(E, ev)
        for w in writes:
            for x in [w] + w.al:
                for ev in x.w:
                    self._wait(E, ev)
                for ev in x.rs:
                    self._wait(E, ev)
        for w in pwrites:
            if w.pw_open and not w.rs:
                for ev in w.pre:
                    self._wait(E, ev)
            else:
                pre = []
                for x in [w] + w.al:
                    pre += x.w + x.rs
                pre = _compact(pre)
                for ev in pre:
                    self._wait(E, ev)
                w.pre_new = pre

    def _mark(self, ev, reads, writes, append_write=False, pwrites=()):
        for w in pwrites:
            if w.pw_open and not w.rs:
                w.w.append(ev)
                if len(w.w) > 16:
                    w.w = _compact(w.w)
            else:
                w.w = [ev]
                w.rs = []
                w.pw_open = True
                w.pre = w.pre_new
        for r in reads:
            r.rs.append(ev)
            r.pw_open = False
            if len(r.rs) > 16:
                r.rs = _compact(r.rs)
        for w in writes:
            if append_write:
                w.w.append(ev)
                if len(w.w) > 16:
                    w.w = _compact(w.w)
            else:
                w.w = [ev]
                w.rs = []
                w.pw_open = False

    def op(self, E, fns, reads=(), writes=(), pw=()):
        if self.dry:
            return None
        if callable(fns):
            fns = [fns]
        self.nops += 1
        if self.nops > self.maxops:
            return None
        self._deps(E, reads, writes, pw)
        self.cnt[E] += 1
        sem = self.sem[E]
        ev = Ev(sem, self.cnt[E])
        q = self.ops[E]
        for f in fns[:-1]:
            q.append(f)
        last = fns[-1]
        q.append(lambda eng: last(eng).then_inc(sem, 1))
        self._mark(ev, reads, writes, pwrites=pw)
        return ev

    def dma(self, Q, out_ap, in_ap, reads=(), writes=(), dsem_owner=None, append_write=False, final=False,
            after=()):
        if self.dry:
            return None
        self.nops += 1
        if self.nops > self.maxops:
            return None
        owner = dsem_owner
        if owner is None:
            owner = writes[0] if (writes and not append_write) else reads[0]
        if owner.dsem is None:
            owner.dsem = DSem(self.new_sem("d_" + owner.name))
        dsem = owner.dsem
        self._deps(Q, reads, writes)
        for ev in after:
            if ev is not None:
                self._wait(Q, ev)
        if dsem.cnt > 0:
            self._wait(Q, Ev(dsem.sem, dsem.cnt))
        dsem.cnt += 16
        ev = Ev(dsem.sem, dsem.cnt)
        sem = dsem.sem
        self.ops[Q].append(lambda eng: eng.dma_start(out=out_ap, in_=in_ap).then_inc(sem, 16))
        self._mark(ev, reads, writes, append_write=append_write)
        if final:
            self.final_evs.append(ev)
            if len(self.final_evs) > 32:
                self.final_evs = _compact(self.final_evs)
        return ev

    def finish(self):
        if self.dry:
            return
        for ev in _compact(self.final_evs):
            self._wait("sp", ev)
        nc = self.nc
        ops = self.ops
        with nc.Block() as block:
            @block.tensor
            def _(eng):
                for f in ops["pe"]:
                    f(eng)

            @block.scalar
            def _(eng):
                for f in ops["act"]:
                    f(eng)

            @block.vector
            def _(eng):
                for f in ops["dve"]:
                    f(eng)

            @block.gpsimd
            def _(eng):
                for f in ops["pool"]:
                    f(eng)

            @block.sync
            def _(eng):
                for f in ops["sp"]:
                    f(eng)
        self.es.close()


class Stream:
    def __init__(self, b, sched, resolver):
        self.b = b
        self.sched = sched
        self.rec = []
        self.resolver = resolver
        self.slots = [b.sb("ring%d" % i, [128, 2048], BF16) for i in range(NBUF)]
        self.free = list(range(NBUF))
        self.loaded = {}
        self.next_load = 0
        self.t = 0

    def _pump(self):
        while self.next_load < len(self.sched) and self.free:
            i = self.next_load
            si = self.free.pop(0)
            s = self.slots[si]
            dst, src, reads = self.resolver(self.sched[i], s)
            self.b.dma("sp", dst, src, reads=reads, writes=[s])
            self.loaded[i] = si
            self.next_load += 1

    def get(self, desc):
        if self.b.dry:
            self.rec.append(desc)
            return (len(self.rec) - 1, T(None))
        self._pump()
        i = self.t
        assert self.sched[i] == desc, (i, self.sched[i], desc)
        assert i in self.loaded, "slot pool too small: tile %d not loadable (%s)" % (i, desc)
        self.t += 1
        return (i, self.slots[self.loaded[i]])

    def release(self, h):
        if self.b.dry:
            return
        self.free.append(self.loaded.pop(h[0]))
        self._pump()


def program(b, st, cfg):
    nqb = cfg["nqb"]

    xkv = [b.dram("xa", [SEQ, D], F32, "ExternalInput"), b.dram("xb", [SEQ, D], F32, "ExternalInput")]
    xq = [xkv[0], b.dram("xqb", [1024, D], F32, "ExternalInput")]
    mem = [b.dram("mema", [256, D], F32, "ExternalInput"), b.dram("memb", [256, D], F32, "ExternalInput")]
    wall = b.dram("wall", [NS, 128, 2048], F32, "ExternalInput")
    colsd = b.dram("cols", [128, 64], F32, "ExternalInput")
    gfd = b.dram("gf", [128, D], F32, "ExternalInput")
    cmat = b.dram("cmat", [3, 128, 128], F32, "ExternalInput")
    r64 = b.dram("r64", [2, 64, SEQ], F32, "ExternalInput")
    r128 = b.dram("r128", [2, 128, SEQ], F32, "ExternalInput")
    r64b = b.dram("r64b", [2, 64, 1024], F32, "ExternalInput")
    r128b = b.dram("r128b", [2, 128, 1024], F32, "ExternalInput")
    yout = [b.dram("ya", [SEQ, D], F32, "ExternalOutput"), b.dram("yb", [1024, D], F32, "ExternalOutput")]
    wbf = b.dram("wbf", [NS, 128, 2048], BF16)
    kmla = b.dram("kmla", [2, 8, 128, SEQ], BF16)
    kped = b.dram("kped", [2, 64, SEQ], BF16)
    vmla = b.dram("vmla", [2, 4, SEQ, 256], BF16)
    kgqa = b.dram("kgqa", [2, 2, 128, SEQ], BF16)
    vgqa = b.dram("vgqa", [2, SEQ, 256], BF16)
    kmemd = b.dram("kmemd", [2, 128, 2048], BF16)
    vmemd = b.dram("vmemd", [2, 128, 2048], BF16)
    kvT = [T(None, "kvA"), T(None, "kvB")]
    conv_chunks = [(0, 2), (2, 5), (5, 9), (9, 17), (17, 25)] + [(s, min(NS, s + CONV_CH)) for s in range(25, NS, CONV_CH)]
    nconv = len(conv_chunks)
    convT = [T(None, "conv%d" % i) for i in range(nconv)]
    slot2conv = {}
    for ci, (s0, s1) in enumerate(conv_chunks):
        for s in range(s0, s1):
            slot2conv[s] = ci
    convsem = T(None, "convsem")

    def resolver(desc, s):
        kind = desc[0]
        if kind == "w":
            i = desc[1]
            return s[:, :], wbf[i], [convT[slot2conv[i]]]
        if kind == "k":
            _, seq, h, half = desc
            return s[:, :], kmla[seq, h][:, half * 2048:(half + 1) * 2048], [kvT[seq]]
        if kind == "kg":
            _, seq, g, half = desc
            return s[:, :], kgqa[seq, g][:, half * 2048:(half + 1) * 2048], [kvT[seq]]
        if kind == "vp":
            _, seq, hp, j = desc
            src = vmla[seq, hp].rearrange("(kt p) c -> p kt c", p=128)[:, 8 * j:8 * j + 8, :]
            return s[:, :].rearrange("p (a c) -> p a c", c=256), src, [kvT[seq]]
        if kind == "vg":
            _, seq, j = desc
            src = vgqa[seq].rearrange("(kt p) c -> p kt c", p=128)[:, 8 * j:8 * j + 8, :]
            return s[:, :].rearrange("p (a c) -> p a c", c=256), src, [kvT[seq]]
        raise ValueError(desc)

    st.resolver = resolver

    big = b.sb("big", [128, 8192], F32)
    xt = [T(big.t, "xt0"), T(big.t, "xt1")]
    big.al = [xt[0], xt[1]]
    xt[0].al = [big]
    xt[1].al = [big]
    xn = b.sb("xn", [128, 2048], BF16)
    xn4T = [T(big.t, "xn4_%d" % i) for i in range(4)]
    for t_ in xn4T:
        t_.al = [big]
        big.al.append(t_)
    xn4 = [big[:, 4096 + i * 1024:4096 + (i + 1) * 1024].bitcast(BF16) for i in range(4)]
    hT = b.sb("hT", [128, 16, 512], BF16)
    qA = b.sb("qA", [128, 8, 512], BF16)
    qP = b.sb("qP", [128, 8, 512], BF16)
    oz = b.sb("oz", [128, 8, 512], BF16)
    cqg = b.sb("cqg", [128, 4, 512], BF16)
    sq4 = b.sb("sq4", [128, 4, 512], BF16)
    PT = [b.sb("pt%d" % i, [128, 512], BF16) for i in range(4)]
    FB = [b.sb("fb%d" % i, [128, 512], F32) for i in range(8)]
    szt = [b.sb("sz%d" % i, [128, 512], F32) for i in range(2)]
    szm = szt + [b.sb("sz%d" % i, [128, 512], F32) for i in range(2, 4)]
    rc64 = [b.sb("rc64_%d" % i, [64, 512], F32) for i in range(2)]
    rc128 = [b.sb("rc128_%d" % i, [128, 512], F32) for i in range(2)]
    kpeT = b.sb("kpeT", [128, SEQ], BF16)
    kmT = b.sb("kmT", [128, 8, 256], BF16)
    vmT = b.sb("vmT", [128, 2, 1024], BF16)
    gF = b.sb("gF", [128, D], F32)
    ident = b.sb("ident", [128, 128], BF16)
    perm64 = b.sb("perm64", [128, 128], F32)
    perm128 = b.sb("perm128", [128, 128], F32)
    ones = b.sb("ones", [128, 4, 128], BF16)
    cols = b.sb("cols", [128, 64], F32)
    epsc = b.sb("epsc", [128, 1], F32)
    onec = b.sb("onec", [128, 1], F32)
    ss = b.sb("ss", [128, 8], F32)
    vsts = [b.sb("vst%d" % i, [128, 1024], BF16) for i in range(2)]
    xsk = b.sb("xsk", [128, 512], F32)
    kgst = b.sb("kgst", [128, 2, 512], BF16)
    vgst = b.sb("vgst", [128, 4, 256], BF16)
    kpst = b.sb("kpst", [64, 512], BF16)
    P = [b.ps("ps%d" % i, [128, 512], F32) for i in range(8)]

    fbi = [0]

    def fb():
        fbi[0] += 1
        return FB[fbi[0] % len(FB)]

    roti = [0]

    def rot():
        roti[0] += 1
        return P[roti[0] % 8]

    def act(out, in_, func, reads, writes=(), pw=(), **kw):
        return b.op("act", I("activation", out=out, in_=in_, func=func, **kw), reads=reads, writes=writes, pw=pw)

    def tt(E, out, in0, in1, op, reads, writes=(), pw=()):
        return b.op(E, I("tensor_tensor", out=out, in0=in0, in1=in1, op=op), reads=reads, writes=writes, pw=pw)

    def ts(E, out, in0, s1, op0, reads, writes=(), pw=()):
        return b.op(E, I("tensor_scalar", out=out, in0=in0, scalar1=s1, scalar2=None, op0=op0),
                    reads=reads, writes=writes, pw=pw)

    def stt(out, in0, scalar, in1, op0, op1, reads, writes=(), pw=()):
        return b.op("dve", I("scalar_tensor_tensor", out=out, in0=in0, scalar=scalar, in1=in1, op0=op0, op1=op1),
                    reads=reads, writes=writes, pw=pw)

    def cp(E, out, in_, reads, writes=(), pw=()):
        if E == "act":
            return act(out, in_, AF.Copy, reads, writes, pw)
        return b.op(E, I("tensor_copy", out=out, in_=in_), reads=reads, writes=writes, pw=pw)

    def mm_group(out_ap, pairs, reads, outT):
        n = len(pairs)
        fns = [I("matmul", out_ap, lhsT=l, rhs=r, start=(i == 0), stop=(i == n - 1)) for i, (l, r) in enumerate(pairs)]
        return b.op("pe", fns, reads=reads, writes=[outT])

    b.dma("sp", cols[:, :], colsd[:, :], writes=[cols])
    b.dma("sp", gF[:, :], gfd[:, :], writes=[gF])
    b.dma("sp", perm64[:, :], cmat[1], writes=[perm64])
    b.dma("sp", perm128[:, :], cmat[2], writes=[perm128])
    b.dma("pool", ident[:, :], cmat[0], writes=[ident])
    for i, v in enumerate((1.0 / 512, 1.0 / 256, 1.0 / 128, 1.0)):
        b.op("dve", I("memset", ones[:, i, :], v), pw=[ones])
    b.op("dve", I("memset", epsc[:, :], EPS), writes=[epsc])
    b.op("dve", I("memset", qP[64:128, :, :], 0.0), writes=[qP])
    b.op("dve", I("memset", kpeT[64:128, :], 0.0), writes=[kpeT])
    b.op("dve", I("memset", onec[:, :], 1.0), writes=[onec])

    stage = cfg.get("stage", 99)
    if stage < 1:
        return
    conv_next = [0]

    def emit_conv(n=1, after=()):
        for _ in range(n):
            i = conv_next[0]
            if i >= nconv:
                return
            s0, s1 = conv_chunks[i]
            b.dma("pool", wbf[s0:s1], wall[s0:s1], writes=[convT[i]], dsem_owner=convsem, after=after)
            conv_next[0] += 1

    emit_conv(5)

    if stage < 2:
        return

    def rstd_from(dst_ap, dstT, src_ap, srcT, n, pw=False):
        t = fb()
        act(t[:, 0:n], src_ap, AF.Ln, reads=[srcT, epsc], writes=[t], bias=epsc[:, 0:1])
        act(dst_ap, t[:, 0:n], AF.Exp, reads=[t], writes=[dstT], scale=-0.5)

    def sigmoid_from(p):
        e = fb()
        act(e[:, :], p[:, :], AF.Exp, reads=[p], writes=[e], scale=-1.0)
        act(e[:, :], e[:, :], AF.Ln, reads=[e, onec], writes=[e], bias=onec[:, 0:1])
        s = fb()
        act(s[:, :], e[:, :], AF.Exp, reads=[e], writes=[s], scale=-1.0)
        return s

    def norm_a(src_ap, i, xnT, xn_ap):
        x = xt[i % 2]
        xo = (i % 2) * 2048
        xap = x[:, xo:xo + 2048]
        b.dma("sp", xap, src_ap, writes=[x])
        act(xn_ap, xap, AF.Square, reads=[x], writes=[xnT, ss], scale=float(1.0 / np.sqrt(2048.0)),
            accum_out=ss[:, 0:1])
        rstd_from(ss[:, 1:2], ss, ss[:, 0:1], ss, 1)
        ts("dve", xn_ap, xap, ss[:, 1:2], ALU.mult, reads=[x, ss], writes=[xnT])

    def norm_b(xnT, xn_ap, gcol0, tk):
        for half in range(2):
            pb = rot()
            pbv = pb[:, :].bitcast(BF16)
            fns = [I("transpose", out=pbv[:, j * 128:(j + 1) * 128],
                     in_=xn_ap[:, (half * 8 + j) * 128:(half * 8 + j + 1) * 128], identity=ident[:, :]) for j in range(8)]
            b.op("pe", fns, reads=[xnT, ident], writes=[pb])
            for j in range(8):
                kc = half * 8 + j
                dst = hT[:, kc, tk * 128:(tk + 1) * 128]
                src = pbv[:, j * 128:(j + 1) * 128]
                g = cols[:, gcol0 + kc:gcol0 + kc + 1]
                if half == 0:
                    act(dst, src, AF.Copy, reads=[pb, cols], pw=[hT], scale=g)
                else:
                    ts("dve", dst, src, g, ALU.mult, reads=[pb, cols], pw=[hT])

    def load_x_and_norm(src_ap, gcol0, tk, i):
        norm_a(src_ap, i, xn, xn[:, :])
        norm_b(xn, xn[:, :], gcol0, tk)

    def proj_lhsT(slot_desc, ncols=128, nk=16, ntok=512):
        h = st.get(slot_desc)
        s = h[1]
        p = rot()
        pairs = [(s[:, kc * ncols:(kc + 1) * ncols], hT[:, kc, 0:ntok]) for kc in range(nk)]
        mm_group(p[0:ncols, 0:ntok], pairs, [s, hT], p)
        st.release(h)
        return p

    def load_rope(seq, is_q, t0):
        if is_q and seq == 1:
            a64, a128 = r64b, r128b
        else:
            a64, a128 = r64, r128
        for i in range(2):
            b.dma("sp", rc64[i][:, :], a64[i][:, t0:t0 + 512], writes=[rc64[i]])
            b.dma("sp", rc128[i][:, :], a128[i][:, t0:t0 + 512], writes=[rc128[i]])

    add_eng = ["dve"]

    def rope(xs, n, dst_ap, dstT, scale_ap=None, scaleT=None):
        perm = perm64 if n == 64 else perm128
        rc = rc64 if n == 64 else rc128
        pr = rot()
        mm_group(pr[0:n, :], [(perm[0:n, 0:n], xs[0:n, :])], [perm, xs], pr)
        t1 = fb()
        tt("dve", t1[0:n, :], xs[0:n, :], rc[0][0:n, :], ALU.mult, reads=[xs, rc[0]], writes=[t1])
        t2 = fb()
        tt("dve", t2[0:n, :], pr[0:n, :], rc[1][0:n, :], ALU.mult, reads=[pr, rc[1]], writes=[t2])
        if scale_ap is None:
            tt(add_eng[0], dst_ap, t1[0:n, :], t2[0:n, :], ALU.add, reads=[t1, t2], pw=[dstT])
        else:
            tt(add_eng[0], t1[0:n, :], t1[0:n, :], t2[0:n, :], ALU.add, reads=[t1, t2], writes=[t1])
            tt("dve", dst_ap, t1[0:n, :], scale_ap, ALU.mult, reads=[t1, scaleT], pw=[dstT])

    def hnr_a(p, gcol, xg, sq):
        act(xg[:, :], p[:, :], AF.Copy, reads=[p, cols], writes=[xg], scale=cols[:, gcol:gcol + 1])
        act(sq[:, :], p[:, :], AF.Square, reads=[p], writes=[sq])

    def hnr_b(xg, sq, dst_ap, dstT):
        pm = rot()
        mm_group(pm[:, :], [(ones[:, 2, :], sq[:, :])], [ones, sq], pm)
        r = fb()
        rstd_from(r[:, :], r, pm[:, :], pm, 512)
        rope(xg, 128, dst_ap, dstT, scale_ap=r[:, :], scaleT=r)

    def head_norm_rope(p, gcol, dst_ap, dstT):
        xg = fb()
        hnr_a(p, gcol, xg, PT[3])
        hnr_b(xg, PT[3], dst_ap, dstT)

    def kv_mem(seq):
        kv = kvT[seq]
        for mt in range(2):
            load_x_and_norm(mem[seq][mt * 128:(mt + 1) * 128, :], 16, mt, mt)
        kmst = oz
        for c in range(8):
            p = proj_lhsT(("w", S_WMK + c), ntok=256)
            cp("act" if c % 2 == 0 else "dve", kmst[:, c, 0:256], p[:, 0:256], reads=[p], pw=[kmst])
        vmst = cqg
        for cb in range(2):
            hs = [st.get(("w", S_WMV + cb * 4 + g)) for g in range(4)]
            for mt in range(2):
                p = rot()
                pairs = [(hT[:, kc, mt * 128:(mt + 1) * 128],
                          hs[kc // 4][1][:, (kc % 4) * 512:(kc % 4 + 1) * 512]) for kc in range(16)]
                mm_group(p[:, :], pairs, [hT] + [h[1] for h in hs], p)
                cp("dve" if mt == 0 else "act", vmst[:, 2 * mt + cb, :], p[:, :], reads=[p], pw=[vmst])
            for h in hs:
                st.release(h)
        b.dma("sp", kmemd[seq].rearrange("p (c m) -> p c m", m=256), kmst[:, :, 0:256],
              reads=[kmst], writes=[kv], append_write=True)
        b.dma("sp", vmemd[seq].rearrange("p (a c) -> p a c", c=512), vmst[:, :, :],
              reads=[vmst], writes=[kv], append_write=True)

    def kv_pass(seq):
        kv = kvT[seq]
        ng = cfg["nkvgrp"]
        ckvn = cqg
        kst = qA
        xgk = szt
        sqk = [PT[2], PT[3]]

        def prep_a(grp, tk):
            t0 = grp * 512
            norm_a(xkv[seq][t0 + tk * 128:t0 + (tk + 1) * 128, :], tk, xn4T[tk], xn4[tk])

        def prep_b(grp, tk):
            norm_b(xn4T[tk], xn4[tk], 0, tk)

        def stage1(grp):
            t0 = grp * 512
            load_rope(seq, False, t0)
            for c in range(2):
                p = proj_lhsT(("w", S_CKV + c))
                act(ckvn[:, c, :], p[:, :], AF.Copy, reads=[p, cols], pw=[ckvn], scale=cols[:, 36 + c:37 + c])
                act(sq4[:, c, :], p[:, :], AF.Square, reads=[p], pw=[sq4])
            for g in range(2):
                p = proj_lhsT(("w", S_KG + g))
                hnr_a(p, 39, xgk[g], sqk[g])
            p = proj_lhsT(("w", S_KPE), ncols=64)
            act(xsk[0:64, :], p[0:64, :], AF.Copy, reads=[p], writes=[xsk])
            hs = [st.get(("w", S_VG + i)) for i in range(2)]
            for tk in range(4):
                p = rot()
                pairs = [(hT[:, kc, tk * 128:(tk + 1) * 128],
                          hs[kc // 8][1][:, (kc % 8) * 256:(kc % 8 + 1) * 256]) for kc in range(16)]
                mm_group(p[:, 0:256], pairs, [hT] + [h[1] for h in hs], p)
                cp("dve", vgst[:, tk, :], p[:, 0:256], reads=[p], pw=[vgst])
            for h in hs:
                st.release(h)
            b.dma("sp", vgqa[seq][t0:t0 + 512, :].rearrange("(a p) c -> p a c", p=128), vgst[:, :, :],
                  reads=[vgst], writes=[kv], append_write=True)
            emit_conv(2 if (seq == 0 and grp == 0) else 1, after=list(vgst.w))

        def stage2_parts(grp):
            t0 = grp * 512
            state = {}

            def kexp(h0, h1):
                hk = state["hk"]
                for h in range(h0, h1):
                    p = rot()
                    pairs = [(hk[1][:, fc * 1024 + h * 128:fc * 1024 + (h + 1) * 128], ckvn[:, fc, :]) for fc in range(2)]
                    mm_group(p[:, :], pairs, [hk[1], ckvn], p)
                    cp("act" if h % 2 == 0 else "dve", kst[:, h, :], p[:, :], reads=[p], pw=[kst])

            def vexp(tk):
                hv = state["hv"]
                vst = vsts[tk % 2]
                for cb in range(2):
                    p = rot()
                    pairs = [(ckvn[:, fc, tk * 128:(tk + 1) * 128],
                              hv[1][:, fc * 1024 + cb * 512:fc * 1024 + (cb + 1) * 512]) for fc in range(2)]
                    mm_group(p[:, :], pairs, [hv[1], ckvn], p)
                    cp("act" if cb == 0 else "dve", vst[:, cb * 512:(cb + 1) * 512], p[:, :], reads=[p], pw=[vst])
                tkk = t0 + tk * 128
                b.dma("sp", vmla[seq].rearrange("hp t c -> t hp c")[tkk:tkk + 128],
                      vst[:, :].rearrange("p (a c) -> p a c", c=256),
                      reads=[vst], writes=[kv], append_write=True)

            def part0():
                pm = rot()
                mm_group(pm[:, :], [(ones[:, 1, :], sq4[:, c, :]) for c in range(2)], [ones, sq4], pm)
                r = fb()
                rstd_from(r[:, :], r, pm[:, :], pm, 512)
                for c in range(2):
                    tt("dve", ckvn[:, c, :], ckvn[:, c, :], r[:, :], ALU.mult, reads=[ckvn, r], writes=[ckvn])
                rope(xsk, 64, kpst[:, :], kpst)
                b.dma("sp", kped[seq][:, t0:t0 + 512], kpst[:, :], reads=[kpst], writes=[kv], append_write=True)
                state["hk"] = st.get(("w", S_WKVB_K))
                kexp(0, 4)

            def part1():
                kexp(4, 8)
                st.release(state["hk"])
                b.dma("sp", kmla[seq].rearrange("h d t -> d h t")[:, :, t0:t0 + 512], kst[:, :, :],
                      reads=[kst], writes=[kv], append_write=True)
                hnr_b(xgk[0], sqk[0], kgst[:, 0, :], kgst)

            def part2():
                state["hv"] = st.get(("w", S_WKVB_V))
                vexp(0)
                vexp(1)
                hnr_b(xgk[1], sqk[1], kgst[:, 1, :], kgst)
                b.dma("sp", kgqa[seq].rearrange("g d t -> d g t")[:, :, t0:t0 + 512], kgst[:, :, :],
                      reads=[kgst], writes=[kv], append_write=True)

            def part3():
                vexp(2)
                vexp(3)
                st.release(state["hv"])

            return [part0, part1, part2, part3]

        for tk in range(4):
            prep_a(0, tk)
        for tk in range(4):
            prep_b(0, tk)
        for grp in range(ng):
            stage1(grp)
            parts = stage2_parts(grp)
            if grp + 1 < ng:
                for tk in range(4):
                    prep_a(grp + 1, tk)
            for k in range(4):
                parts[k]()
                if grp + 1 < ng:
                    prep_b(grp + 1, k)
        if cfg.get("stage", 99) >= 3:
            kv_mem(seq)

    def attention(nkt, qk_pairs, qk_reads, v_lhsT, v_reads, scale, Ob, Sb):
        def qk(kt):
            S = P[kt % 3]
            mm_group(S[:, :], qk_pairs(kt), qk_reads(kt), S)

        def ex(kt):
            S = P[kt % 3]
            pt = PT[kt % 3]
            act(pt[:, :], S[:, :], AF.Exp, reads=[S], writes=[pt], scale=float(scale))

        def pv(kt):
            pt = PT[kt % 3]
            fns = [I("matmul", Ob[:, :], lhsT=v_lhsT(kt), rhs=pt[:, :], start=(kt == 0), stop=(kt == nkt - 1)),
                   I("matmul", Sb[:, :], lhsT=ones[:, 3, :], rhs=pt[:, :], start=(kt == 0), stop=(kt == nkt - 1))]
            b.op("pe", fns, reads=[pt, ones] + v_reads(kt), writes=[Ob, Sb])

        qk(0)
        if nkt > 1:
            qk(1)
        for kt in range(nkt):
            ex(kt)
            if kt + 2 < nkt:
                qk(kt + 2)
            pv(kt)

    def finish_head(Ob, Sb, sz, dst_ap):
        t = fb()
        act(t[:, :], Sb[:, :], AF.Ln, reads=[Sb], writes=[t])
        rec = fb()
        act(rec[:, :], t[:, :], AF.Exp, reads=[t], writes=[rec], scale=-1.0)
        a = fb()
        tt("dve", a[:, :], Ob[:, :], rec[:, :], ALU.mult, reads=[Ob, rec], writes=[a])
        tt("dve", dst_ap, a[:, :], sz[:, :], ALU.mult, reads=[a, sz], pw=[oz])

    def zproj(i, c, sz):
        h = st.get(("w", S_Z[i] + c))
        s = h[1]
        p = P[7]
        mm_group(p[:, :], [(s[:, kc * 128:(kc + 1) * 128], hT[:, kc, :]) for kc in range(16)], [s, hT], p)
        st.release(h)
        sg = sigmoid_from(p)
        tt("dve", sz[:, :], sg[:, :], p[:, :], ALU.mult, reads=[sg, p], writes=[sz])

    def branch_proj(i):
        hb = None
        for cc in range(16):
            if cc % 2 == 0:
                hb = st.get(("w", S_WB[i] + cc // 2))
            sb_ = hb[1]
            pt_ = rot()
            o0 = (cc % 2) * 1024
            mm_group(pt_[:, :], [(sb_[:, o0 + kc * 128:o0 + (kc + 1) * 128], oz[:, kc, :]) for kc in range(8)],
                     [sb_, oz], pt_)
            if cc % 2 == 1:
                st.release(hb)
            pg = proj_lhsT(("w", S_GL[i] + cc))
            sg = sigmoid_from(pg)
            mo = cc * 512
            if i == 0:
                tt("dve", big[:, mo:mo + 512], sg[:, :], pt_[:, :], ALU.mult, reads=[sg, pt_], pw=[big])
            else:
                tm = fb()
                tt("dve", tm[:, :], sg[:, :], pt_[:, :], ALU.mult, reads=[sg, pt_], writes=[tm])
                tt("pool", big[:, mo:mo + 512], big[:, mo:mo + 512], tm[:, :], ALU.add, reads=[tm, big], writes=[big])

    def q_block(seq, qb):
        kv = kvT[seq]
        t0 = qb * 512
        if qb == 0:
            b.dma("sp", kpeT[0:64, :], kped[seq], reads=[kv], writes=[kpeT])
            b.dma("sp", kmT[:, :, :], kmemd[seq].rearrange("p (c m) -> p c m", m=256), reads=[kv], writes=[kmT])
            b.dma("sp", vmT[:, :, :], vmemd[seq].rearrange("p (a c) -> p a c", c=1024), reads=[kv], writes=[vmT])
        load_rope(seq, True, t0)
        for tk in range(4):
            load_x_and_norm(xq[seq][t0 + tk * 128:t0 + (tk + 1) * 128, :], 0, tk, tk)

        cqn = cqg
        for c in range(4):
            p = proj_lhsT(("w", S_CQ + c))
            act(cqn[:, c, :], p[:, :], AF.Copy, reads=[p, cols], pw=[cqn], scale=cols[:, 32 + c:33 + c])
            act(sq4[:, c, :], p[:, :], AF.Square, reads=[p], pw=[sq4])
        pm = rot()
        mm_group(pm[:, :], [(ones[:, 0, :], sq4[:, c, :]) for c in range(4)], [ones, sq4], pm)
        r = fb()
        rstd_from(r[:, :], r, pm[:, :], pm, 512)
        for c in range(4):
            tt("dve", cqn[:, c, :], cqn[:, c, :], r[:, :], ALU.mult, reads=[cqn, r], writes=[cqn])
        hqs = {}

        def q_proj(h):
            if h % 2 == 0:
                hqs[0] = st.get(("w", S_WQB + h // 2))
            s = hqs[0][1]
            base = (h % 2) * 768
            p = rot()
            mm_group(p[:, :], [(s[:, base + fc * 128:base + (fc + 1) * 128], cqn[:, fc, :]) for fc in range(4)],
                     [s, cqn], p)
            cp("act", qA[:, h, :], p[:, :], reads=[p], pw=[qA])
            p2 = rot()
            mm_group(p2[0:64, :], [(s[:, base + 512 + fc * 64:base + 512 + (fc + 1) * 64], cqn[:, fc, :])
                                   for fc in range(4)], [s, cqn], p2)
            if h % 2 == 1:
                st.release(hqs[0])
            xs = fb()
            act(xs[0:64, :], p2[0:64, :], AF.Copy, reads=[p2], writes=[xs])
            return xs

        xs_prev = q_proj(0)
        for h in range(1, 8):
            xs_cur = q_proj(h)
            rope(xs_prev, 64, qP[0:64, h - 1, :], qP)
            xs_prev = xs_cur
        rope(xs_prev, 64, qP[0:64, 7, :], qP)
        sc = 1.0 / np.sqrt(192.0)
        vps = None
        for h in range(8):
            sz = szt[h % 2]
            zproj(0, h, sz)
            k0 = st.get(("k", seq, h, 0))
            if h % 2 == 0:
                vps = [st.get(("vp", seq, h // 2, j)) for j in range(4)]
            k1 = st.get(("k", seq, h, 1))
            ks = [k0, k1]
            Ob, Sb = P[3 + h % 2], P[5 + h % 2]
            attention(
                32,
                lambda kt: [(ks[kt // 16][1][:, (kt % 16) * 128:(kt % 16 + 1) * 128], qA[:, h, :]),
                            (kpeT[:, kt * 128:(kt + 1) * 128], qP[:, h, :])],
                lambda kt: [ks[kt // 16][1], qA, kpeT, qP],
                lambda kt: vps[kt // 8][1][:, (kt % 8) * 256 + (h % 2) * 128:(kt % 8) * 256 + (h % 2 + 1) * 128],
                lambda kt: [vps[kt // 8][1]],
                sc, Ob, Sb)
            st.release(k0)
            st.release(k1)
            if h % 2 == 1:
                for v in vps:
                    st.release(v)
            finish_head(Ob, Sb, sz, oz[:, h, :])
        branch_proj(0)

        def g_a(h):
            p = proj_lhsT(("w", S_QG + h))
            hnr_a(p, 38, szt[h % 2], PT[2 + h % 2])

        g_a(0)
        for h in range(1, 8):
            g_a(h)
            hnr_b(szt[(h - 1) % 2], PT[2 + (h - 1) % 2], qA[:, h - 1, :], qA)
        hnr_b(szt[1], PT[3], qA[:, 7, :], qA)
        sc = 1.0 / np.sqrt(128.0)
        vgs = None
        ks = None
        for g in range(2):
            for hh in range(4):
                h = g * 4 + hh
                sz = szt[h % 2]
                zproj(1, h, sz)
                if hh == 0:
                    k0 = st.get(("kg", seq, g, 0))
                    if g == 0:
                        vgs = [st.get(("vg", seq, j)) for j in range(4)]
                    k1 = st.get(("kg", seq, g, 1))
                    ks = [k0, k1]
                Ob, Sb = P[3 + h % 2], P[5 + h % 2]
                attention(
                    32,
                    lambda kt: [(ks[kt // 16][1][:, (kt % 16) * 128:(kt % 16 + 1) * 128], qA[:, h, :])],
                    lambda kt: [ks[kt // 16][1], qA],
                    lambda kt: vgs[kt // 8][1][:, (kt % 8) * 256 + g * 128:(kt % 8) * 256 + (g + 1) * 128],
                    lambda kt: [vgs[kt // 8][1]],
                    sc, Ob, Sb)
                finish_head(Ob, Sb, sz, oz[:, h, :])
            st.release(ks[0])
            st.release(ks[1])
        for v in vgs:
            st.release(v)
        branch_proj(1)

        for c in range(8):
            p = proj_lhsT(("w", S_QM + c))
            cp("act" if c % 2 == 0 else "dve", qA[:, c, :], p[:, :], reads=[p], pw=[qA])
        sc = 1.0 / np.sqrt(256.0)
        SM = [[P[0], P[1]], [P[2], P[5]]]

        def m_a(h):
            for dvc in range(2):
                zproj(2, 2 * h + dvc, szm[(h % 2) * 2 + dvc])
            for mt in range(2):
                S = SM[h % 2][mt]
                mm_group(S[:, :], [(kmT[:, 2 * h + dc, mt * 128:(mt + 1) * 128], qA[:, 2 * h + dc, :])
                                   for dc in range(2)], [kmT, qA], S)
                pt = PT[(h % 2) * 2 + mt]
                act(pt[:, :], S[:, :], AF.Exp, reads=[S], writes=[pt], scale=float(sc))

        def m_b(h):
            pts = [PT[(h % 2) * 2 + mt] for mt in range(2)]
            Sb = P[6]
            mm_group(Sb[:, :], [(ones[:, 3, :], pts[mt][:, :]) for mt in range(2)], [ones] + pts, Sb)
            for dvc in range(2):
                Ob = P[3 + dvc]
                mm_group(Ob[:, :], [(vmT[:, mt, h * 256 + dvc * 128:h * 256 + (dvc + 1) * 128], pts[mt][:, :])
                                    for mt in range(2)], [vmT] + pts, Ob)
                finish_head(Ob, Sb, szm[(h % 2) * 2 + dvc], oz[:, 2 * h + dvc, :])

        m_a(0)
        for h in range(1, 4):
            m_a(h)
            m_b(h - 1)
        m_b(3)
        branch_proj(2)

        mbf = hT
        for cc in range(16):
            cp("dve" if cc % 2 == 0 else "act", mbf[:, cc, :], big[:, cc * 512:(cc + 1) * 512], reads=[big], pw=[mbf])

        for pair in range(2):
            for j in range(2):
                tk = pair * 2 + j
                x = xt[j]
                b.dma("sp", x[:, j * 2048:(j + 1) * 2048], xq[seq][t0 + tk * 128:t0 + (tk + 1) * 128, :], writes=[x])
            for cb in range(4):
                hs = [st.get(("w", S_WO + cb * 4 + g)) for g in range(4)]
                for j in range(2):
                    tk = pair * 2 + j
                    x = xt[j]
                    p = rot()
                    pairs = [(mbf[:, kc, tk * 128:(tk + 1) * 128],
                              hs[kc // 4][1][:, (kc % 4) * 512:(kc % 4 + 1) * 512]) for kc in range(16)]
                    mm_group(p[:, :], pairs, [mbf] + [h[1] for h in hs], p)
                    xa = x[:, j * 2048 + cb * 512:j * 2048 + (cb + 1) * 512]
                    tt("dve", xa, p[:, :], xa, ALU.add, reads=[p, x], writes=[x])
                for h in hs:
                    st.release(h)
            for j in range(2):
                tk = pair * 2 + j
                x = xt[j]
                xa = x[:, j * 2048:(j + 1) * 2048]
                c0 = 2 + 2 * j
                act(xn[:, :], xa, AF.Square, reads=[x], writes=[xn, ss], scale=float(1.0 / np.sqrt(2048.0)),
                    accum_out=ss[:, c0:c0 + 1])
                rstd_from(ss[:, c0 + 1:c0 + 2], ss, ss[:, c0:c0 + 1], ss, 1)
                stt(xa, xa, ss[:, c0 + 1:c0 + 2], gF[:, :], ALU.mult, ALU.mult, reads=[x, ss, gF], writes=[x])
                b.dma("sp", yout[seq][t0 + tk * 128:t0 + (tk + 1) * 128, :], xa, reads=[x], final=True,
                      dsem_owner=x)

    for seq in cfg["kvseqs"]:
        kv_pass(seq)
    emit_conv(nconv)
    add_eng[0] = "pool"
    for seq in range(2):
        for qb in range(nqb[seq]):
            q_block(seq, qb)


FULL_CFG = {"nqb": [8, 2], "kvseqs": [0, 1], "nkvgrp": 8}
_NC_CACHE = {}


def build_nc(cfg=FULL_CFG):
    bd = B(None, dry=True)
    std = Stream(bd, None, None)
    program(bd, std, cfg)
    sched = std.rec
    nc = bass.Bass("TRN2", target_bir_lowering=False)
    b = B(nc)
    b.maxops = cfg.get("maxops", 10 ** 9)
    st = Stream(b, sched, None)
    program(b, st, cfg)
    assert st.t == len(sched), (st.t, len(sched))
    print("nops", b.nops, "nsem", b.nsem)
    b.finish()
    return nc


def _lhsT_tile(W, c0, ncols):
    K = W.shape[0]
    nk = K // 128
    t = W[:, c0:c0 + ncols].reshape(nk, 128, ncols).transpose(1, 0, 2).reshape(128, nk * ncols)
    return t


def _rhs_tile(W, c0, ncols, k0, nk):
    t = W[:, c0:c0 + ncols].reshape(-1, 128, ncols)[k0:k0 + nk].transpose(1, 0, 2).reshape(128, nk * ncols)
    return t


def _build_wall(w_in, w_q_b, w_kv_b, w_mem_kv, w_branch, w_out):
    wall = np.zeros((NS, 128, 2048), np.float32)

    def put(s, t):
        wall[s, :, :t.shape[1]] = t

    CQ0, CKV0, KPE0, QG0, KG0, VG0, QM0, Z0, GL0 = 0, 512, 768, 832, 1856, 2112, 2368, 3392, 6464
    for c in range(2):
        put(S_CKV + c, _lhsT_tile(w_in, CKV0 + c * 128, 128))
        put(S_KG + c, _lhsT_tile(w_in, KG0 + c * 128, 128))
    put(S_KPE, _lhsT_tile(w_in, KPE0, 64))
    for i in range(2):
        put(S_VG + i, _rhs_tile(w_in, VG0, 256, 8 * i, 8))
    kvb = w_kv_b.reshape(2, 128, 8, 256)
    put(S_WKVB_K, kvb[:, :, :, :128].transpose(1, 0, 2, 3).reshape(128, 2048))
    put(S_WKVB_V, kvb[:, :, :, 128:].transpose(1, 0, 2, 3).reshape(128, 2048))
    for c in range(8):
        put(S_WMK + c, _lhsT_tile(w_mem_kv, c * 128, 128))
    for cb in range(2):
        for g in range(4):
            put(S_WMV + cb * 4 + g, _rhs_tile(w_mem_kv, 1024 + cb * 512, 512, 4 * g, 4))
    for c in range(4):
        put(S_CQ + c, _lhsT_tile(w_in, CQ0 + c * 128, 128))
    for j in range(4):
        for hl in range(2):
            h = 2 * j + hl
            wall[S_WQB + j, :, hl * 768:hl * 768 + 512] = _lhsT_tile(w_q_b, h * 192, 128)
            wall[S_WQB + j, :, hl * 768 + 512:hl * 768 + 768] = _lhsT_tile(w_q_b, h * 192 + 128, 64)
    for i in range(3):
        for c in range(8):
            put(S_Z[i] + c, _lhsT_tile(w_in, Z0 + i * 1024 + c * 128, 128))
            t = np.concatenate([_lhsT_tile(w_branch[i], (2 * c + half) * 128, 128) for half in range(2)], axis=1)
            put(S_WB[i] + c, t)
        for cc in range(16):
            put(S_GL[i] + cc, _lhsT_tile(w_in, GL0 + i * 2048 + cc * 128, 128))
    for h in range(8):
        put(S_QG + h, _lhsT_tile(w_in, QG0 + h * 128, 128))
        put(S_QM + h, _lhsT_tile(w_in, QM0 + h * 128, 128))
    for cb in range(4):
        for g in range(4):
            put(S_WO + cb * 4 + g, _rhs_tile(w_out, cb * 512, 512, 4 * g, 4))
    return wall


def _perm(n, half):
    m = np.zeros((128, 128), np.float32)
    blk = 2 * half
    for j in range(n):
        if j % blk < half:
            m[j + half, j] = -1.0
        else:
            m[j - half, j] = 1.0
    return m


def _rope_table(n, half):
    t = np.arange(SEQ)
    row = (t // 64).astype(np.float32)
    col = (t % 64).astype(np.float32)
    freqs = (np.float32(10000.0) ** (-(np.arange(half, dtype=np.float32) / np.float32(half)))).astype(np.float32)
    out = np.zeros((2, n, SEQ), np.float32)
    for p in range(n):
        pos = row if p < n // 2 else col
        ang = (pos * freqs[p % half]).astype(np.float32)
        out[0, p] = np.cos(ang)
        out[1, p] = np.sin(ang)
    return out


def make_in_maps(x_prompt, x_sample, mem_prompt, mem_sample, norm_g, w_in, mla_q_norm_g, w_q_b,
                 mla_kv_norm_g, w_kv_b, gqa_q_norm_g, gqa_k_norm_g, mem_norm_g, w_mem_kv,
                 w_branch, w_out, final_norm_g, cores=range(8)):
    f = lambda a: np.ascontiguousarray(np.asarray(a, dtype=np.float32))
    wall = _build_wall(f(w_in)[0], f(w_q_b)[0], f(w_kv_b)[0], f(w_mem_kv)[0], f(w_branch)[0], f(w_out)[0])
    cols = np.zeros((128, 64), np.float32)
    cols[:, 0:16] = f(norm_g)[0].reshape(16, 128).T
    cols[:, 16:32] = f(mem_norm_g)[0].reshape(16, 128).T
    cols[:, 32:36] = f(mla_q_norm_g)[0].reshape(4, 128).T
    cols[:, 36:38] = f(mla_kv_norm_g)[0].reshape(2, 128).T
    cols[:, 38] = f(gqa_q_norm_g)[0]
    cols[:, 39] = f(gqa_k_norm_g)[0]
    gf = np.ascontiguousarray(np.broadcast_to(f(final_norm_g)[None, :], (128, D)))
    cmat = np.stack([np.eye(128, dtype=np.float32), _perm(64, 16), _perm(128, 32)])
    r64 = _rope_table(64, 16)
    r128 = _rope_table(128, 32)
    x_prompt, x_sample, mem_prompt, mem_sample = f(x_prompt), f(x_sample), f(mem_prompt), f(mem_sample)
    maps = []
    for c in cores:
        sb, ch = c // 4, c % 4
        off = ch * 1024
        maps.append({
            "xa": x_prompt[c], "xb": x_sample[sb], "xqb": np.ascontiguousarray(x_sample[sb, off:off + 1024]),
            "mema": mem_prompt[c], "memb": mem_sample[sb], "wall": wall, "cols": cols, "gf": gf, "cmat": cmat,
            "r64": r64, "r128": r128,
            "r64b": np.ascontiguousarray(r64[:, :, off:off + 1024]),
            "r128b": np.ascontiguousarray(r128[:, :, off:off + 1024]),
        })
    return maps


def kernel(**inputs):
    maps = make_in_maps(**inputs)
    if "nc" not in _NC_CACHE:
        _NC_CACHE["nc"] = build_nc(FULL_CFG)
    res = run_bass_kernel_spmd(_NC_CACHE["nc"], maps, core_ids=list(range(8)))
    y_prompt = np.zeros((8, SEQ, D), np.float32)
    y_sample = np.zeros((2, SEQ, D), np.float32)
    for c in range(8):
        r = res.results[c]
        y_prompt[c] = r["ya"]
        y_sample[c // 4, (c % 4) * 1024:(c % 4 + 1) * 1024] = r["yb"]
    return (y_prompt, y_sample)
```

```python
import contextlib
import numpy as np
import concourse.bass as bass
import concourse.mybir as mybir
from concourse.bass_utils import run_bass_kernel_spmd

F32 = mybir.dt.float32
BF16 = mybir.dt.bfloat16
AF = mybir.ActivationFunctionType
ALU = mybir.AluOpType

ENGS = ("pe", "act", "dve", "pool", "sp")
EPS = 1e-6
NBUF = 12
SEQ = 4096
D = 2048

S_CKV, S_KG, S_KPE, S_VG, S_WKVB_K, S_WKVB_V = 0, 2, 4, 5, 7, 8
S_WMK, S_WMV = 9, 17
S_CQ, S_WQB = 25, 29
S_Z = [33, 73, 113]
S_WB = [41, 81, 121]
S_GL = [49, 89, 129]
S_QG, S_QM = 65, 105
S_WO = 145
NS = 161
CONV_CH = 8


class Ev:
    __slots__ = ("sem", "val")

    def __init__(self, sem, val):
        self.sem = sem
        self.val = val


class _Dummy:
    def __getitem__(self, idx):
        return self

    def __getattr__(self, name):
        return self

    def __call__(self, *a, **k):
        return self


_DUMMY = _Dummy()


def I(name, *args, **kwargs):
    return lambda eng: getattr(eng, name)(*args, **kwargs)


class T:
    def __init__(self, t, name=""):
        self.t = t
        self.name = name
        self.w = []
        self.rs = []
        self.dsem = None
        self.al = []
        self.pw_open = False
        self.psum = False
        self.pre = []
        self.pre_new = []

    def __getitem__(self, idx):
        if self.t is None:
            return _DUMMY
        return self.t[idx]


class DSem:
    def __init__(self, sem):
        self.sem = sem
        self.cnt = 0


def _compact(evs):
    best = {}
    for ev in evs:
        k = id(ev.sem)
        if k not in best or best[k].val < ev.val:
            best[k] = ev
    return list(best.values())


class B:
    def __init__(self, nc, dry=False):
        self.nc = nc
        self.dry = dry
        self.ops = {e: [] for e in ENGS}
        self.es = contextlib.ExitStack()
        self.sem = {}
        self.cnt = {e: 0 for e in ENGS}
        self.known = {e: {} for e in ENGS}
        self.final_evs = []
        self.nsem = 0
        self.maxops = 10 ** 9
        self.nops = 0
        if not dry:
            for e in ("pe", "act", "dve", "pool"):
                self.sem[e] = self.new_sem("p_" + e)

    def new_sem(self, name):
        self.nsem += 1
        return self.es.enter_context(self.nc.semaphore(name))

    def sb(self, name, shape, dtype):
        if self.dry:
            return T(None, name)
        return T(self.es.enter_context(self.nc.sbuf_tensor("s_" + name, list(shape), dtype)), name)

    def ps(self, name, shape, dtype):
        if self.dry:
            return T(None, name)
        t = T(self.es.enter_context(self.nc.psum_tensor("p_" + name, list(shape), dtype)), name)
        t.psum = True
        return t

    def dram(self, name, shape, dtype, kind=None):
        if self.dry:
            return T(None, name)
        if kind is None:
            t = self.nc.dram_tensor(name, list(shape), dtype)
        else:
            t = self.nc.dram_tensor(name, list(shape), dtype, kind=kind)
        return T(t.ap(), name)

    def _wait(self, E, ev):
        if E == "pe" and ev.sem is self.sem.get("pe"):
            return
        k = self.known[E]
        sid = id(ev.sem)
        if k.get(sid, 0) >= ev.val:
            return
        k[sid] = ev.val
        sem, val = ev.sem, ev.val
        self.ops[E].append(lambda eng: eng.wait_ge(sem, val))

    def _deps(self, E, reads, writes, pwrites=()):
        for r in reads:
            for ev in r.w:
                self._wait(E, ev)
            if r.psum:
                own = self.sem.get(E)
                for ev in r.rs:
                    if ev.sem is not own:
                        self._wait(E, ev)
            for a in r.al:
                for ev in a.w:
                    self._wait(E, ev)
        for w in writes:
            for x in [w] + w.al:
                for ev in x.w:
                    self._wait(E, ev)
                for ev in x.rs:
                    self._wait(E, ev)
        for w in pwrites:
            if w.pw_open and not w.rs:
                for ev in w.pre:
                    self._wait(E, ev)
            else:
                pre = []
                for x in [w] + w.al:
                    pre += x.w + x.rs
                pre = _compact(pre)
                for ev in pre:
                    self._wait(E, ev)
                w.pre_new = pre

    def _mark(self, ev, reads, writes, append_write=False, pwrites=()):
        for w in pwrites:
            if w.pw_open and not w.rs:
                w.w.append(ev)
                if len(w.w) > 16:
                    w.w = _compact(w.w)
            else:
                w.w = [ev]
                w.rs = []
                w.pw_open = True
                w.pre = w.pre_new
        for r in reads:
            r.rs.append(ev)
            r.pw_open = False
            if len(r.rs) > 16:
                r.rs = _compact(r.rs)
        for w in writes:
            if append_write:
                w.w.append(ev)
                if len(w.w) > 16:
                    w.w = _compact(w.w)
            else:
                w.w = [ev]
                w.rs = []
                w.pw_open = False

    def op(self, E, fns, reads=(), writes=(), pw=()):
        if self.dry:
            return None
        if callable(fns):
            fns = [fns]
        self.nops += 1
        if self.nops > self.maxops:
            return None
        self._deps(E, reads, writes, pw)
        self.cnt[E] += 1
        sem = self.sem[E]
        ev = Ev(sem, self.cnt[E])
        q = self.ops[E]
        for f in fns[:-1]:
            q.append(f)
        last = fns[-1]
        q.append(lambda eng: last(eng).then_inc(sem, 1))
        self._mark(ev, reads, writes, pwrites=pw)
        return ev

    def dma(self, Q, out_ap, in_ap, reads=(), writes=(), dsem_owner=None, append_write=False, final=False,
            after=()):
        if self.dry:
            return None
        self.nops += 1
        if self.nops > self.maxops:
            return None
        owner = dsem_owner
        if owner is None:
            owner = writes[0] if (writes and not append_write) else reads[0]
        if owner.dsem is None:
            owner.dsem = DSem(self.new_sem("d_" + owner.name))
        dsem = owner.dsem
        self._deps(Q, reads, writes)
        for ev in after:
            if ev is not None:
                self._wait(Q, ev)
        if dsem.cnt > 0:
            self._wait(Q, Ev(dsem.sem, dsem.cnt))
        dsem.cnt += 16
        ev = Ev(dsem.sem, dsem.cnt)
        sem = dsem.sem
        self.ops[Q].append(lambda eng: eng.dma_start(out=out_ap, in_=in_ap).then_inc(sem, 16))
        self._mark(ev, reads, writes, append_write=append_write)
        if final:
            self.final_evs.append(ev)
            if len(self.final_evs) > 32:
                self.final_evs = _compact(self.final_evs)
        return ev

    def finish(self):
        if self.dry:
            return
        for ev in _compact(self.final_evs):
            self._wait("sp", ev)
        nc = self.nc
        ops = self.ops
        with nc.Block() as block:
            @block.tensor
            def _(eng):
                for f in ops["pe"]:
                    f(eng)

            @block.scalar
            def _(eng):
                for f in ops["act"]:
                    f(eng)

            @block.vector
            def _(eng):
                for f in ops["dve"]:
                    f(eng)

            @block.gpsimd
            def _(eng):
                for f in ops["pool"]:
                    f(eng)

            @block.sync
            def _(eng):
                for f in ops["sp"]:
                    f(eng)
        self.es.close()


class Stream:
    def __init__(self, b, sched, resolver):
        self.b = b
        self.sched = sched
        self.rec = []
        self.resolver = resolver
        self.slots = [b.sb("ring%d" % i, [128, 2048], BF16) for i in range(NBUF)]
        self.free = list(range(NBUF))
        self.loaded = {}
        self.next_load = 0
        self.t = 0

    def _pump(self):
        while self.next_load < len(self.sched) and self.free:
            i = self.next_load
            si = self.free.pop(0)
            s = self.slots[si]
            dst, src, reads = self.resolver(self.sched[i], s)
            self.b.dma("sp", dst, src, reads=reads, writes=[s])
            self.loaded[i] = si
            self.next_load += 1

    def get(self, desc):
        if self.b.dry:
            self.rec.append(desc)
            return (len(self.rec) - 1, T(None))
        self._pump()
        i = self.t
        assert self.sched[i] == desc, (i, self.sched[i], desc)
        assert i in self.loaded, "slot pool too small: tile %d not loadable (%s)" % (i, desc)
        self.t += 1
        return (i, self.slots[self.loaded[i]])

    def release(self, h):
        if self.b.dry:
            return
        self.free.append(self.loaded.pop(h[0]))
        self._pump()


def program(b, st, cfg):
    nqb = cfg["nqb"]

    xkv = [b.dram("xa", [SEQ, D], F32, "ExternalInput"), b.dram("xb", [SEQ, D], F32, "ExternalInput")]
    xq = [xkv[0], b.dram("xqb", [1024, D], F32, "ExternalInput")]
    mem = [b.dram("mema", [256, D], F32, "ExternalInput"), b.dram("memb", [256, D], F32, "ExternalInput")]
    wall = b.dram("wall", [NS, 128, 2048], F32, "ExternalInput")
    colsd = b.dram("cols", [128, 64], F32, "ExternalInput")
    gfd = b.dram("gf", [128, D], F32, "ExternalInput")
    cmat = b.dram("cmat", [3, 128, 128], F32, "ExternalInput")
    r64 = b.dram("r64", [2, 64, SEQ], F32, "ExternalInput")
    r128 = b.dram("r128", [2, 128, SEQ], F32, "ExternalInput")
    r64b = b.dram("r64b", [2, 64, 1024], F32, "ExternalInput")
    r128b = b.dram("r128b", [2, 128, 1024], F32, "ExternalInput")
    yout = [b.dram("ya", [SEQ, D], F32, "ExternalOutput"), b.dram("yb", [1024, D], F32, "ExternalOutput")]
    wbf = b.dram("wbf", [NS, 128, 2048], BF16)
    kmla = b.dram("kmla", [2, 8, 128, SEQ], BF16)
    kped = b.dram("kped", [2, 64, SEQ], BF16)
    vmla = b.dram("vmla", [2, 4, SEQ, 256], BF16)
    kgqa = b.dram("kgqa", [2, 2, 128, SEQ], BF16)
    vgqa = b.dram("vgqa", [2, SEQ, 256], BF16)
    kmemd = b.dram("kmemd", [2, 128, 2048], BF16)
    vmemd = b.dram("vmemd", [2, 128, 2048], BF16)
    kvT = [T(None, "kvA"), T(None, "kvB")]
    conv_chunks = [(0, 2), (2, 5), (5, 9), (9, 17), (17, 25)] + [(s, min(NS, s + CONV_CH)) for s in range(25, NS, CONV_CH)]
    nconv = len(conv_chunks)
    convT = [T(None, "conv%d" % i) for i in range(nconv)]
    slot2conv = {}
    for ci, (s0, s1) in enumerate(conv_chunks):
        for s in range(s0, s1):
            slot2conv[s] = ci
    convsem = T(None, "convsem")

    def resolver(desc, s):
        kind = desc[0]
        if kind == "w":
            i = desc[1]
            return s[:, :], wbf[i], [convT[slot2conv[i]]]
        if kind == "k":
            _, seq, h, half = desc
            return s[:, :], kmla[seq, h][:, half * 2048:(half + 1) * 2048], [kvT[seq]]
        if kind == "kg":
            _, seq, g, half = desc
            return s[:, :], kgqa[seq, g][:, half * 2048:(half + 1) * 2048], [kvT[seq]]
        if kind == "vp":
            _, seq, hp, j = desc
            src = vmla[seq, hp].rearrange("(kt p) c -> p kt c", p=128)[:, 8 * j:8 * j + 8, :]
            return s[:, :].rearrange("p (a c) -> p a c", c=256), src, [kvT[seq]]
        if kind == "vg":
            _, seq, j = desc
            src = vgqa[seq].rearrange("(kt p) c -> p kt c", p=128)[:, 8 * j:8 * j + 8, :]
            return s[:, :].rearrange("p (a c) -> p a c", c=256), src, [kvT[seq]]
        raise ValueError(desc)

    st.resolver = resolver

    big = b.sb("big", [128, 8192], F32)
    xt = [T(big.t, "xt0"), T(big.t, "xt1")]
    big.al = [xt[0], xt[1]]
    xt[0].al = [big]
    xt[1].al = [big]
    xn = b.sb("xn", [128, 2048], BF16)
    xn4T = [T(big.t, "xn4_%d" % i) for i in range(4)]
    for t_ in xn4T:
        t_.al = [big]
        big.al.append(t_)
    xn4 = [big[:, 4096 + i * 1024:4096 + (i + 1) * 1024].bitcast(BF16) for i in range(4)]
    hT = b.sb("hT", [128, 16, 512], BF16)
    qA = b.sb("qA", [128, 8, 512], BF16)
    qP = b.sb("qP", [128, 8, 512], BF16)
    oz = b.sb("oz", [128, 8, 512], BF16)
    cqg = b.sb("cqg", [128, 4, 512], BF16)
    sq4 = b.sb("sq4", [128, 4, 512], BF16)
    PT = [b.sb("pt%d" % i, [128, 512], BF16) for i in range(4)]
    FB = [b.sb("fb%d" % i, [128, 512], F32) for i in range(8)]
    szt = [b.sb("sz%d" % i, [128, 512], F32) for i in range(2)]
    szm = szt + [b.sb("sz%d" % i, [128, 512], F32) for i in range(2, 4)]
    rc64 = [b.sb("rc64_%d" % i, [64, 512], F32) for i in range(2)]
    rc128 = [b.sb("rc128_%d" % i, [128, 512], F32) for i in range(2)]
    kpeT = b.sb("kpeT", [128, SEQ], BF16)
    kmT = b.sb("kmT", [128, 8, 256], BF16)
    vmT = b.sb("vmT", [128, 2, 1024], BF16)
    gF = b.sb("gF", [128, D], F32)
    ident = b.sb("ident", [128, 128], BF16)
    perm64 = b.sb("perm64", [128, 128], F32)
    perm128 = b.sb("perm128", [128, 128], F32)
    ones = b.sb("ones", [128, 4, 128], BF16)
    cols = b.sb("cols", [128, 64], F32)
    epsc = b.sb("epsc", [128, 1], F32)
    onec = b.sb("onec", [128, 1], F32)
    ss = b.sb("ss", [128, 8], F32)
    vsts = [b.sb("vst%d" % i, [128, 1024], BF16) for i in range(2)]
    xsk = b.sb("xsk", [128, 512], F32)
    kgst = b.sb("kgst", [128, 2, 512], BF16)
    vgst = b.sb("vgst", [128, 4, 256], BF16)
    kpst = b.sb("kpst", [64, 512], BF16)
    P = [b.ps("ps%d" % i, [128, 512], F32) for i in range(8)]

    fbi = [0]

    def fb():
        fbi[0] += 1
        return FB[fbi[0] % len(FB)]

    roti = [0]

    def rot():
        roti[0] += 1
        return P[roti[0] % 8]

    def act(out, in_, func, reads, writes=(), pw=(), **kw):
        return b.op("act", I("activation", out=out, in_=in_, func=func, **kw), reads=reads, writes=writes, pw=pw)

    def tt(E, out, in0, in1, op, reads, writes=(), pw=()):
        return b.op(E, I("tensor_tensor", out=out, in0=in0, in1=in1, op=op), reads=reads, writes=writes, pw=pw)

    def ts(E, out, in0, s1, op0, reads, writes=(), pw=()):
        return b.op(E, I("tensor_scalar", out=out, in0=in0, scalar1=s1, scalar2=None, op0=op0),
                    reads=reads, writes=writes, pw=pw)

    def stt(out, in0, scalar, in1, op0, op1, reads, writes=(), pw=()):
        return b.op("dve", I("scalar_tensor_tensor", out=out, in0=in0, scalar=scalar, in1=in1, op0=op0, op1=op1),
                    reads=reads, writes=writes, pw=pw)

    def cp(E, out, in_, reads, writes=(), pw=()):
        if E == "act":
            return act(out, in_, AF.Copy, reads, writes, pw)
        return b.op(E, I("tensor_copy", out=out, in_=in_), reads=reads, writes=writes, pw=pw)

    def mm_group(out_ap, pairs, reads, outT):
        n = len(pairs)
        fns = [I("matmul", out_ap, lhsT=l, rhs=r, start=(i == 0), stop=(i == n - 1)) for i, (l, r) in enumerate(pairs)]
        return b.op("pe", fns, reads=reads, writes=[outT])

    b.dma("sp", cols[:, :], colsd[:, :], writes=[cols])
    b.dma("sp", gF[:, :], gfd[:, :], writes=[gF])
    b.dma("sp", perm64[:, :], cmat[1], writes=[perm64])
    b.dma("sp", perm128[:, :], cmat[2], writes=[perm128])
    b.dma("pool", ident[:, :], cmat[0], writes=[ident])
    for i, v in enumerate((1.0 / 512, 1.0 / 256, 1.0 / 128, 1.0)):
        b.op("dve", I("memset", ones[:, i, :], v), pw=[ones])
    b.op("dve", I("memset", epsc[:, :], EPS), writes=[epsc])
    b.op("dve", I("memset", qP[64:128, :, :], 0.0), writes=[qP])
    b.op("dve", I("memset", kpeT[64:128, :], 0.0), writes=[kpeT])
    b.op("dve", I("memset", onec[:, :], 1.0), writes=[onec])

    stage = cfg.get("stage", 99)
    if stage < 1:
        return
    conv_next = [0]

    def emit_conv(n=1, after=()):
        for _ in range(n):
            i = conv_next[0]
            if i >= nconv:
                return
            s0, s1 = conv_chunks[i]
            b.dma("pool", wbf[s0:s1], wall[s0:s1], writes=[convT[i]], dsem_owner=convsem, after=after)
            conv_next[0] += 1

    emit_conv(5)

    if stage < 2:
        return

    def rstd_from(dst_ap, dstT, src_ap, srcT, n, pw=False):
        t = fb()
        act(t[:, 0:n], src_ap, AF.Ln, reads=[srcT, epsc], writes=[t], bias=epsc[:, 0:1])
        act(dst_ap, t[:, 0:n], AF.Exp, reads=[t], writes=[dstT], scale=-0.5)

    def sigmoid_from(p):
        e = fb()
        act(e[:, :], p[:, :], AF.Exp, reads=[p], writes=[e], scale=-1.0)
        act(e[:, :], e[:, :], AF.Ln, reads=[e, onec], writes=[e], bias=onec[:, 0:1])
        s = fb()
        act(s[:, :], e[:, :], AF.Exp, reads=[e], writes=[s], scale=-1.0)
        return s

    def norm_a(src_ap, i, xnT, xn_ap):
        x = xt[i % 2]
        xo = (i % 2) * 2048
        xap = x[:, xo:xo + 2048]
        b.dma("sp", xap, src_ap, writes=[x])
        act(xn_ap, xap, AF.Square, reads=[x], writes=[xnT, ss], scale=float(1.0 / np.sqrt(2048.0)),
            accum_out=ss[:, 0:1])
        rstd_from(ss[:, 1:2], ss, ss[:, 0:1], ss, 1)
        ts("dve", xn_ap, xap, ss[:, 1:2], ALU.mult, reads=[x, ss], writes=[xnT])

    def norm_b(xnT, xn_ap, gcol0, tk):
        for half in range(2):
            pb = rot()
            pbv = pb[:, :].bitcast(BF16)
            fns = [I("transpose", out=pbv[:, j * 128:(j + 1) * 128],
                     in_=xn_ap[:, (half * 8 + j) * 128:(half * 8 + j + 1) * 128], identity=ident[:, :]) for j in range(8)]
            b.op("pe", fns, reads=[xnT, ident], writes=[pb])
            for j in range(8):
                kc = half * 8 + j
                dst = hT[:, kc, tk * 128:(tk + 1) * 128]
                src = pbv[:, j * 128:(j + 1) * 128]
                g = cols[:, gcol0 + kc:gcol0 + kc + 1]
                if half == 0:
                    act(dst, src, AF.Copy, reads=[pb, cols], pw=[hT], scale=g)
                else:
                    ts("dve", dst, src, g, ALU.mult, reads=[pb, cols], pw=[hT])

    def load_x_and_norm(src_ap, gcol0, tk, i):
        norm_a(src_ap, i, xn, xn[:, :])
        norm_b(xn, xn[:, :], gcol0, tk)

    def proj_lhsT(slot_desc, ncols=128, nk=16, ntok=512):
        h = st.get(slot_desc)
        s = h[1]
        p = rot()
        pairs = [(s[:, kc * ncols:(kc + 1) * ncols], hT[:, kc, 0:ntok]) for kc in range(nk)]
        mm_group(p[0:ncols, 0:ntok], pairs, [s, hT], p)
        st.release(h)
        return p

    def load_rope(seq, is_q, t0):
        if is_q and seq == 1:
            a64, a128 = r64b, r128b
        else:
            a64, a128 = r64, r128
        for i in range(2):
            b.dma("sp", rc64[i][:, :], a64[i][:, t0:t0 + 512], writes=[rc64[i]])
            b.dma("sp", rc128[i][:, :], a128[i][:, t0:t0 + 512], writes=[rc128[i]])

    add_eng = ["dve"]

    def rope(xs, n, dst_ap, dstT, scale_ap=None, scaleT=None):
        perm = perm64 if n == 64 else perm128
        rc = rc64 if n == 64 else rc128
        pr = rot()
        mm_group(pr[0:n, :], [(perm[0:n, 0:n], xs[0:n, :])], [perm, xs], pr)
        t1 = fb()
        tt("dve", t1[0:n, :], xs[0:n, :], rc[0][0:n, :], ALU.mult, reads=[xs, rc[0]], writes=[t1])
        t2 = fb()
        tt("dve", t2[0:n, :], pr[0:n, :], rc[1][0:n, :], ALU.mult, reads=[pr, rc[1]], writes=[t2])
        if scale_ap is None:
            tt(add_eng[0], dst_ap, t1[0:n, :], t2[0:n, :], ALU.add, reads=[t1, t2], pw=[dstT])
        else:
            tt(add_eng[0], t1[0:n, :], t1[0:n, :], t2[0:n, :], ALU.add, reads=[t1, t2], writes=[t1])
            tt("dve", dst_ap, t1[0:n, :], scale_ap, ALU.mult, reads=[t1, scaleT], pw=[dstT])

    def hnr_a(p, gcol, xg, sq):
        act(xg[:, :], p[:, :], AF.Copy, reads=[p, cols], writes=[xg], scale=cols[:, gcol:gcol + 1])
        act(sq[:, :], p[:, :], AF.Square, reads=[p], writes=[sq])

    def hnr_b(xg, sq, dst_ap, dstT):
        pm = rot()
        mm_group(pm[:, :], [(ones[:, 2, :], sq[:, :])], [ones, sq], pm)
        r = fb()
        rstd_from(r[:, :], r, pm[:, :], pm, 512)
        rope(xg, 128, dst_ap, dstT, scale_ap=r[:, :], scaleT=r)

    def head_norm_rope(p, gcol, dst_ap, dstT):
        xg = fb()
        hnr_a(p, gcol, xg, PT[3])
        hnr_b(xg, PT[3], dst_ap, dstT)

    def kv_mem(seq):
        kv = kvT[seq]
        for mt in range(2):
            load_x_and_norm(mem[seq][mt * 128:(mt + 1) * 128, :], 16, mt, mt)
        kmst = oz
        for c in range(8):
            p = proj_lhsT(("w", S_WMK + c), ntok=256)
            cp("act" if c % 2 == 0 else "dve", kmst[:, c, 0:256], p[:, 0:256], reads=[p], pw=[kmst])
        vmst = cqg
        for cb in range(2):
            hs = [st.get(("w", S_WMV + cb * 4 + g)) for g in range(4)]
            for mt in range(2):
                p = rot()
                pairs = [(hT[:, kc, mt * 128:(mt + 1) * 128],
                          hs[kc // 4][1][:, (kc % 4) * 512:(kc % 4 + 1) * 512]) for kc in range(16)]
                mm_group(p[:, :], pairs, [hT] + [h[1] for h in hs], p)
                cp("dve" if mt == 0 else "act", vmst[:, 2 * mt + cb, :], p[:, :], reads=[p], pw=[vmst])
            for h in hs:
                st.release(h)
        b.dma("sp", kmemd[seq].rearrange("p (c m) -> p c m", m=256), kmst[:, :, 0:256],
              reads=[kmst], writes=[kv], append_write=True)
        b.dma("sp", vmemd[seq].rearrange("p (a c) -> p a c", c=512), vmst[:, :, :],
              reads=[vmst], writes=[kv], append_write=True)

    def kv_pass(seq):
        kv = kvT[seq]
        ng = cfg["nkvgrp"]
        ckvn = cqg
        kst = qA
        xgk = szt
        sqk = [PT[2], PT[3]]

        def prep_a(grp, tk):
            t0 = grp * 512
            norm_a(xkv[seq][t0 + tk * 128:t0 + (tk + 1) * 128, :], tk, xn4T[tk], xn4[tk])

        def prep_b(grp, tk):
            norm_b(xn4T[tk], xn4[tk], 0, tk)

        def stage1(grp):
            t0 = grp * 512
            load_rope(seq, False, t0)
            for c in range(2):
                p = proj_lhsT(("w", S_CKV + c))
                act(ckvn[:, c, :], p[:, :], AF.Copy, reads=[p, cols], pw=[ckvn], scale=cols[:, 36 + c:37 + c])
                act(sq4[:, c, :], p[:, :], AF.Square, reads=[p], pw=[sq4])
            for g in range(2):
                p = proj_lhsT(("w", S_KG + g))
                hnr_a(p, 39, xgk[g], sqk[g])
            p = proj_lhsT(("w", S_KPE), ncols=64)
            act(xsk[0:64, :], p[0:64, :], AF.Copy, reads=[p], writes=[xsk])
            hs = [st.get(("w", S_VG + i)) for i in range(2)]
            for tk in range(4):
                p = rot()
                pairs = [(hT[:, kc, tk * 128:(tk + 1) * 128],
                          hs[kc // 8][1][:, (kc % 8) * 256:(kc % 8 + 1) * 256]) for kc in range(16)]
                mm_group(p[:, 0:256], pairs, [hT] + [h[1] for h in hs], p)
                cp("dve", vgst[:, tk, :], p[:, 0:256], reads=[p], pw=[vgst])
            for h in hs:
                st.release(h)
            b.dma("sp", vgqa[seq][t0:t0 + 512, :].rearrange("(a p) c -> p a c", p=128), vgst[:, :, :],
                  reads=[vgst], writes=[kv], append_write=True)
            emit_conv(2 if (seq == 0 and grp == 0) else 1, after=list(vgst.w))

        def stage2_parts(grp):
            t0 = grp * 512
            state = {}

            def kexp(h0, h1):
                hk = state["hk"]
                for h in range(h0, h1):
                    p = rot()
                    pairs = [(hk[1][:, fc * 1024 + h * 128:fc * 1024 + (h + 1) * 128], ckvn[:, fc, :]) for fc in range(2)]
                    mm_group(p[:, :], pairs, [hk[1], ckvn], p)
                    cp("act" if h % 2 == 0 else "dve", kst[:, h, :], p[:, :], reads=[p], pw=[kst])

            def vexp(tk):
                hv = state["hv"]
                vst = vsts[tk % 2]
                for cb in range(2):
                    p = rot()
                    pairs = [(ckvn[:, fc, tk * 128:(tk + 1) * 128],
                              hv[1][:, fc * 1024 + cb * 512:fc * 1024 + (cb + 1) * 512]) for fc in range(2)]
                    mm_group(p[:, :], pairs, [hv[1], ckvn], p)
                    cp("act" if cb == 0 else "dve", vst[:, cb * 512:(cb + 1) * 512], p[:, :], reads=[p], pw=[vst])
                tkk = t0 + tk * 128
                b.dma("sp", vmla[seq].rearrange("hp t c -> t hp c")[tkk:tkk + 128],
                      vst[:, :].rearrange("p (a c) -> p a c", c=256),
                      reads=[vst], writes=[kv], append_write=True)

            def part0():
                pm = rot()
                mm_group(pm[:, :], [(ones[:, 1, :], sq4[:, c, :]) for c in range(2)], [ones, sq4], pm)
                r = fb()
                rstd_from(r[:, :], r, pm[:, :], pm, 512)
                for c in range(2):
                    tt("dve", ckvn[:, c, :], ckvn[:, c, :], r[:, :], ALU.mult, reads=[ckvn, r], writes=[ckvn])
                rope(xsk, 64, kpst[:, :], kpst)
                b.dma("sp", kped[seq][:, t0:t0 + 512], kpst[:, :], reads=[kpst], writes=[kv], append_write=True)
                state["hk"] = st.get(("w", S_WKVB_K))
                kexp(0, 4)

            def part1():
                kexp(4, 8)
                st.release(state["hk"])
                b.dma("sp", kmla[seq].rearrange("h d t -> d h t")[:, :, t0:t0 + 512], kst[:, :, :],
                      reads=[kst], writes=[kv], append_write=True)
                hnr_b(xgk[0], sqk[0], kgst[:, 0, :], kgst)

            def part2():
                state["hv"] = st.get(("w", S_WKVB_V))
                vexp(0)
                vexp(1)
                hnr_b(xgk[1], sqk[1], kgst[:, 1, :], kgst)
                b.dma("sp", kgqa[seq].rearrange("g d t -> d g t")[:, :, t0:t0 + 512], kgst[:, :, :],
                      reads=[kgst], writes=[kv], append_write=True)

            def part3():
                vexp(2)
                vexp(3)
                st.release(state["hv"])

            return [part0, part1, part2, part3]

        for tk in range(4):
            prep_a(0, tk)
        for tk in range(4):
            prep_b(0, tk)
        for grp in range(ng):
            stage1(grp)
            parts = stage2_parts(grp)
            if grp + 1 < ng:
                for tk in range(4):
                    prep_a(grp + 1, tk)
            for k in range(4):
                parts[k]()
                if grp + 1 < ng:
                    prep_b(grp + 1, k)
        if cfg.get("stage", 99) >= 3:
            kv_mem(seq)

    UT = [b.sb("ut%d" % i, [128, 512], BF16) for i in range(2)]

    def attention(nkt, qk_pairs, qk_reads, v_lhsT, v_reads, scale, Ob, Sb):
        def qk(kt):
            S = P[kt % 3]
            mm_group(S[:, :], qk_pairs(kt), qk_reads(kt), S)

        def ex(kt):
            S = P[kt % 3]
            pt = PT[kt % 4]
            act(pt[:, :], S[:, :], AF.Exp, reads=[S], writes=[pt], scale=float(scale))

        def pv(kt):
            pt = PT[kt % 4]
            b.op("pe", I("matmul", Ob[:, :], lhsT=v_lhsT(kt), rhs=pt[:, :], start=(kt == 0), stop=(kt == nkt - 1)),
                 reads=[pt] + v_reads(kt), writes=[Ob])

        def rs(kt):
            u = UT[(kt // 2) % 2]
            tt("pool", u[:, :], PT[(kt - 1) % 4][:, :], PT[kt % 4][:, :], ALU.add,
               reads=[PT[(kt - 1) % 4], PT[kt % 4]], writes=[u])
            return u

        def rsmm(kt, u):
            b.op("pe", I("matmul", Sb[:, :], lhsT=ones[:, 3, :], rhs=u[:, :], start=(kt == 1), stop=(kt == nkt - 1)),
                 reads=[u, ones], writes=[Sb])

        qk(0)
        qk(1)
        pend_u = None
        for kt in range(nkt):
            ex(kt)
            if kt + 2 < nkt:
                qk(kt + 2)
            pv(kt)
            if pend_u is not None:
                rsmm(*pend_u)
                pend_u = None
            if kt % 2 == 1:
                pend_u = (kt, rs(kt))
        rsmm(*pend_u)

    def finish_head(Ob, Sb, sz, dst_ap):
        t = fb()
        act(t[:, :], Sb[:, :], AF.Ln, reads=[Sb], writes=[t])
        rec = fb()
        act(rec[:, :], t[:, :], AF.Exp, reads=[t], writes=[rec], scale=-1.0)
        a = fb()
        tt("dve", a[:, :], Ob[:, :], rec[:, :], ALU.mult, reads=[Ob, rec], writes=[a])
        tt("dve", dst_ap, a[:, :], sz[:, :], ALU.mult, reads=[a, sz], pw=[oz])

    def zproj(i, c, sz):
        h = st.get(("w", S_Z[i] + c))
        s = h[1]
        p = P[7]
        mm_group(p[:, :], [(s[:, kc * 128:(kc + 1) * 128], hT[:, kc, :]) for kc in range(16)], [s, hT], p)
        st.release(h)
        sg = sigmoid_from(p)
        tt("dve", sz[:, :], sg[:, :], p[:, :], ALU.mult, reads=[sg, p], writes=[sz])

    def branch_proj(i):
        hb = None
        for cc in range(16):
            if cc % 2 == 0:
                hb = st.get(("w", S_WB[i] + cc // 2))
            sb_ = hb[1]
            pt_ = rot()
            o0 = (cc % 2) * 1024
            mm_group(pt_[:, :], [(sb_[:, o0 + kc * 128:o0 + (kc + 1) * 128], oz[:, kc, :]) for kc in range(8)],
                     [sb_, oz], pt_)
            if cc % 2 == 1:
                st.release(hb)
            pg = proj_lhsT(("w", S_GL[i] + cc))
            sg = sigmoid_from(pg)
            mo = cc * 512
            if i == 0:
                tt("dve", big[:, mo:mo + 512], sg[:, :], pt_[:, :], ALU.mult, reads=[sg, pt_], pw=[big])
            else:
                tm = fb()
                tt("dve", tm[:, :], sg[:, :], pt_[:, :], ALU.mult, reads=[sg, pt_], writes=[tm])
                tt("pool", big[:, mo:mo + 512], big[:, mo:mo + 512], tm[:, :], ALU.add, reads=[tm, big], writes=[big])

    def q_block(seq, qb):
        kv = kvT[seq]
        t0 = qb * 512
        if qb == 0:
            b.dma("sp", kpeT[0:64, :], kped[seq], reads=[kv], writes=[kpeT])
            b.dma("sp", kmT[:, :, :], kmemd[seq].rearrange("p (c m) -> p c m", m=256), reads=[kv], writes=[kmT])
            b.dma("sp", vmT[:, :, :], vmemd[seq].rearrange("p (a c) -> p a c", c=1024), reads=[kv], writes=[vmT])
        load_rope(seq, True, t0)
        for tk in range(4):
            load_x_and_norm(xq[seq][t0 + tk * 128:t0 + (tk + 1) * 128, :], 0, tk, tk)

        cqn = cqg
        for c in range(4):
            p = proj_lhsT(("w", S_CQ + c))
            act(cqn[:, c, :], p[:, :], AF.Copy, reads=[p, cols], pw=[cqn], scale=cols[:, 32 + c:33 + c])
            act(sq4[:, c, :], p[:, :], AF.Square, reads=[p], pw=[sq4])
        pm = rot()
        mm_group(pm[:, :], [(ones[:, 0, :], sq4[:, c, :]) for c in range(4)], [ones, sq4], pm)
        r = fb()
        rstd_from(r[:, :], r, pm[:, :], pm, 512)
        for c in range(4):
            tt("dve", cqn[:, c, :], cqn[:, c, :], r[:, :], ALU.mult, reads=[cqn, r], writes=[cqn])
        hqs = {}

        def q_proj(h):
            if h % 2 == 0:
                hqs[0] = st.get(("w", S_WQB + h // 2))
            s = hqs[0][1]
            base = (h % 2) * 768
            p = rot()
            mm_group(p[:, :], [(s[:, base + fc * 128:base + (fc + 1) * 128], cqn[:, fc, :]) for fc in range(4)],
                     [s, cqn], p)
            cp("act", qA[:, h, :], p[:, :], reads=[p], pw=[qA])
            p2 = rot()
            mm_group(p2[0:64, :], [(s[:, base + 512 + fc * 64:base + 512 + (fc + 1) * 64], cqn[:, fc, :])
                                   for fc in range(4)], [s, cqn], p2)
            if h % 2 == 1:
                st.release(hqs[0])
            xs = fb()
            act(xs[0:64, :], p2[0:64, :], AF.Copy, reads=[p2], writes=[xs])
            return xs

        xs_prev = q_proj(0)
        for h in range(1, 8):
            xs_cur = q_proj(h)
            rope(xs_prev, 64, qP[0:64, h - 1, :], qP)
            xs_prev = xs_cur
        rope(xs_prev, 64, qP[0:64, 7, :], qP)
        sc = 1.0 / np.sqrt(192.0)
        vps = None
        for h in range(8):
            sz = szt[h % 2]
            zproj(0, h, sz)
            k0 = st.get(("k", seq, h, 0))
            if h % 2 == 0:
                vps = [st.get(("vp", seq, h // 2, j)) for j in range(4)]
            k1 = st.get(("k", seq, h, 1))
            ks = [k0, k1]
            Ob, Sb = P[3 + h % 2], P[5 + h % 2]
            attention(
                32,
                lambda kt: [(ks[kt // 16][1][:, (kt % 16) * 128:(kt % 16 + 1) * 128], qA[:, h, :]),
                            (kpeT[:, kt * 128:(kt + 1) * 128], qP[:, h, :])],
                lambda kt: [ks[kt // 16][1], qA, kpeT, qP],
                lambda kt: vps[kt // 8][1][:, (kt % 8) * 256 + (h % 2) * 128:(kt % 8) * 256 + (h % 2 + 1) * 128],
                lambda kt: [vps[kt // 8][1]],
                sc, Ob, Sb)
            st.release(k0)
            st.release(k1)
            if h % 2 == 1:
                for v in vps:
                    st.release(v)
            finish_head(Ob, Sb, sz, oz[:, h, :])
        branch_proj(0)

        def g_a(h):
            p = proj_lhsT(("w", S_QG + h))
            hnr_a(p, 38, szt[h % 2], PT[2 + h % 2])

        g_a(0)
        for h in range(1, 8):
            g_a(h)
            hnr_b(szt[(h - 1) % 2], PT[2 + (h - 1) % 2], qA[:, h - 1, :], qA)
        hnr_b(szt[1], PT[3], qA[:, 7, :], qA)
        sc = 1.0 / np.sqrt(128.0)
        vgs = None
        ks = None
        for g in range(2):
            for hh in range(4):
                h = g * 4 + hh
                sz = szt[h % 2]
                zproj(1, h, sz)
                if hh == 0:
                    k0 = st.get(("kg", seq, g, 0))
                    if g == 0:
                        vgs = [st.get(("vg", seq, j)) for j in range(4)]
                    k1 = st.get(("kg", seq, g, 1))
                    ks = [k0, k1]
                Ob, Sb = P[3 + h % 2], P[5 + h % 2]
                attention(
                    32,
                    lambda kt: [(ks[kt // 16][1][:, (kt % 16) * 128:(kt % 16 + 1) * 128], qA[:, h, :])],
                    lambda kt: [ks[kt // 16][1], qA],
                    lambda kt: vgs[kt // 8][1][:, (kt % 8) * 256 + g * 128:(kt % 8) * 256 + (g + 1) * 128],
                    lambda kt: [vgs[kt // 8][1]],
                    sc, Ob, Sb)
                finish_head(Ob, Sb, sz, oz[:, h, :])
            st.release(ks[0])
            st.release(ks[1])
        for v in vgs:
            st.release(v)
        branch_proj(1)

        for c in range(8):
            p = proj_lhsT(("w", S_QM + c))
            cp("act" if c % 2 == 0 else "dve", qA[:, c, :], p[:, :], reads=[p], pw=[qA])
        sc = 1.0 / np.sqrt(256.0)
        SM = [[P[0], P[1]], [P[2], P[5]]]

        def m_a(h):
            for dvc in range(2):
                zproj(2, 2 * h + dvc, szm[(h % 2) * 2 + dvc])
            for mt in range(2):
                S = SM[h % 2][mt]
                mm_group(S[:, :], [(kmT[:, 2 * h + dc, mt * 128:(mt + 1) * 128], qA[:, 2 * h + dc, :])
                                   for dc in range(2)], [kmT, qA], S)
                pt = PT[(h % 2) * 2 + mt]
                act(pt[:, :], S[:, :], AF.Exp, reads=[S], writes=[pt], scale=float(sc))

        def m_b(h):
            pts = [PT[(h % 2) * 2 + mt] for mt in range(2)]
            Sb = P[6]
            mm_group(Sb[:, :], [(ones[:, 3, :], pts[mt][:, :]) for mt in range(2)], [ones] + pts, Sb)
            for dvc in range(2):
                Ob = P[3 + dvc]
                mm_group(Ob[:, :], [(vmT[:, mt, h * 256 + dvc * 128:h * 256 + (dvc + 1) * 128], pts[mt][:, :])
                                    for mt in range(2)], [vmT] + pts, Ob)
                finish_head(Ob, Sb, szm[(h % 2) * 2 + dvc], oz[:, 2 * h + dvc, :])

        m_a(0)
        for h in range(1, 4):
            m_a(h)
            m_b(h - 1)
        m_b(3)
        branch_proj(2)

        mbf = hT
        for cc in range(16):
            cp("dve" if cc % 2 == 0 else "act", mbf[:, cc, :], big[:, cc * 512:(cc + 1) * 512], reads=[big], pw=[mbf])

        for pair in range(2):
            for j in range(2):
                tk = pair * 2 + j
                x = xt[j]
                b.dma("sp", x[:, j * 2048:(j + 1) * 2048], xq[seq][t0 + tk * 128:t0 + (tk + 1) * 128, :], writes=[x])
            for cb in range(4):
                hs = [st.get(("w", S_WO + cb * 4 + g)) for g in range(4)]
                for j in range(2):
                    tk = pair * 2 + j
                    x = xt[j]
                    p = rot()
                    pairs = [(mbf[:, kc, tk * 128:(tk + 1) * 128],
                              hs[kc // 4][1][:, (kc % 4) * 512:(kc % 4 + 1) * 512]) for kc in range(16)]
                    mm_group(p[:, :], pairs, [mbf] + [h[1] for h in hs], p)
                    xa = x[:, j * 2048 + cb * 512:j * 2048 + (cb + 1) * 512]
                    tt("dve", xa, p[:, :], xa, ALU.add, reads=[p, x], writes=[x])
                for h in hs:
                    st.release(h)
            for j in range(2):
                tk = pair * 2 + j
                x = xt[j]
                xa = x[:, j * 2048:(j + 1) * 2048]
                c0 = 2 + 2 * j
                act(xn[:, :], xa, AF.Square, reads=[x], writes=[xn, ss], scale=float(1.0 / np.sqrt(2048.0)),
                    accum_out=ss[:, c0:c0 + 1])
                rstd_from(ss[:, c0 + 1:c0 + 2], ss, ss[:, c0:c0 + 1], ss, 1)
                stt(xa, xa, ss[:, c0 + 1:c0 + 2], gF[:, :], ALU.mult, ALU.mult, reads=[x, ss, gF], writes=[x])
                b.dma("sp", yout[seq][t0 + tk * 128:t0 + (tk + 1) * 128, :], xa, reads=[x], final=True,
                      dsem_owner=x)

    for seq in cfg["kvseqs"]:
        kv_pass(seq)
    emit_conv(nconv)
    add_eng[0] = "pool"
    for seq in range(2):
        for qb in range(nqb[seq]):
            q_block(seq, qb)


FULL_CFG = {"nqb": [8, 2], "kvseqs": [0, 1], "nkvgrp": 8}
_NC_CACHE = {}


def build_nc(cfg=FULL_CFG):
    bd = B(None, dry=True)
    std = Stream(bd, None, None)
    program(bd, std, cfg)
    sched = std.rec
    nc = bass.Bass("TRN2", target_bir_lowering=False)
    b = B(nc)
    b.maxops = cfg.get("maxops", 10 ** 9)
    st = Stream(b, sched, None)
    program(b, st, cfg)
    assert st.t == len(sched), (st.t, len(sched))
    print("nops", b.nops, "nsem", b.nsem)
    b.finish()
    return nc


def _lhsT_tile(W, c0, ncols):
    K = W.shape[0]
    nk = K // 128
    t = W[:, c0:c0 + ncols].reshape(nk, 128, ncols).transpose(1, 0, 2).reshape(128, nk * ncols)
    return t


def _rhs_tile(W, c0, ncols, k0, nk):
    t = W[:, c0:c0 + ncols].reshape(-1, 128, ncols)[k0:k0 + nk].transpose(1, 0, 2).reshape(128, nk * ncols)
    return t


def _build_wall(w_in, w_q_b, w_kv_b, w_mem_kv, w_branch, w_out):
    wall = np.zeros((NS, 128, 2048), np.float32)

    def put(s, t):
        wall[s, :, :t.shape[1]] = t

    CQ0, CKV0, KPE0, QG0, KG0, VG0, QM0, Z0, GL0 = 0, 512, 768, 832, 1856, 2112, 2368, 3392, 6464
    for c in range(2):
        put(S_CKV + c, _lhsT_tile(w_in, CKV0 + c * 128, 128))
        put(S_KG + c, _lhsT_tile(w_in, KG0 + c * 128, 128))
    put(S_KPE, _lhsT_tile(w_in, KPE0, 64))
    for i in range(2):
        put(S_VG + i, _rhs_tile(w_in, VG0, 256, 8 * i, 8))
    kvb = w_kv_b.reshape(2, 128, 8, 256)
    put(S_WKVB_K, kvb[:, :, :, :128].transpose(1, 0, 2, 3).reshape(128, 2048))
    put(S_WKVB_V, kvb[:, :, :, 128:].transpose(1, 0, 2, 3).reshape(128, 2048))
    for c in range(8):
        put(S_WMK + c, _lhsT_tile(w_mem_kv, c * 128, 128))
    for cb in range(2):
        for g in range(4):
            put(S_WMV + cb * 4 + g, _rhs_tile(w_mem_kv, 1024 + cb * 512, 512, 4 * g, 4))
    for c in range(4):
        put(S_CQ + c, _lhsT_tile(w_in, CQ0 + c * 128, 128))
    for j in range(4):
        for hl in range(2):
            h = 2 * j + hl
            wall[S_WQB + j, :, hl * 768:hl * 768 + 512] = _lhsT_tile(w_q_b, h * 192, 128)
            wall[S_WQB + j, :, hl * 768 + 512:hl * 768 + 768] = _lhsT_tile(w_q_b, h * 192 + 128, 64)
    for i in range(3):
        for c in range(8):
            put(S_Z[i] + c, _lhsT_tile(w_in, Z0 + i * 1024 + c * 128, 128))
            t = np.concatenate([_lhsT_tile(w_branch[i], (2 * c + half) * 128, 128) for half in range(2)], axis=1)
            put(S_WB[i] + c, t)
        for cc in range(16):
            put(S_GL[i] + cc, _lhsT_tile(w_in, GL0 + i * 2048 + cc * 128, 128))
    for h in range(8):
        put(S_QG + h, _lhsT_tile(w_in, QG0 + h * 128, 128))
        put(S_QM + h, _lhsT_tile(w_in, QM0 + h * 128, 128))
    for cb in range(4):
        for g in range(4):
            put(S_WO + cb * 4 + g, _rhs_tile(w_out, cb * 512, 512, 4 * g, 4))
    return wall


def _perm(n, half):
    m = np.zeros((128, 128), np.float32)
    blk = 2 * half
    for j in range(n):
        if j % blk < half:
            m[j + half, j] = -1.0
        else:
            m[j - half, j] = 1.0
    return m


def _rope_table(n, half):
    t = np.arange(SEQ)
    row = (t // 64).astype(np.float32)
    col = (t % 64).astype(np.float32)
    freqs = (np.float32(10000.0) ** (-(np.arange(half, dtype=np.float32) / np.float32(half)))).astype(np.float32)
    out = np.zeros((2, n, SEQ), np.float32)
    for p in range(n):
        pos = row if p < n // 2 else col
        ang = (pos * freqs[p % half]).astype(np.float32)
        out[0, p] = np.cos(ang)
        out[1, p] = np.sin(ang)
    return out


def make_in_maps(x_prompt, x_sample, mem_prompt, mem_sample, norm_g, w_in, mla_q_norm_g, w_q_b,
                 mla_kv_norm_g, w_kv_b, gqa_q_norm_g, gqa_k_norm_g, mem_norm_g, w_mem_kv,
                 w_branch, w_out, final_norm_g, cores=range(8)):
    f = lambda a: np.ascontiguousarray(np.asarray(a, dtype=np.float32))
    wall = _build_wall(f(w_in)[0], f(w_q_b)[0], f(w_kv_b)[0], f(w_mem_kv)[0], f(w_branch)[0], f(w_out)[0])
    cols = np.zeros((128, 64), np.float32)
    cols[:, 0:16] = f(norm_g)[0].reshape(16, 128).T
    cols[:, 16:32] = f(mem_norm_g)[0].reshape(16, 128).T
    cols[:, 32:36] = f(mla_q_norm_g)[0].reshape(4, 128).T
    cols[:, 36:38] = f(mla_kv_norm_g)[0].reshape(2, 128).T
    cols[:, 38] = f(gqa_q_norm_g)[0]
    cols[:, 39] = f(gqa_k_norm_g)[0]
    gf = np.ascontiguousarray(np.broadcast_to(f(final_norm_g)[None, :], (128, D)))
    cmat = np.stack([np.eye(128, dtype=np.float32), _perm(64, 16), _perm(128, 32)])
    r64 = _rope_table(64, 16)
    r128 = _rope_table(128, 32)
    x_prompt, x_sample, mem_prompt, mem_sample = f(x_prompt), f(x_sample), f(mem_prompt), f(mem_sample)
    maps = []
    for c in cores:
        sb, ch = c // 4, c % 4
        off = ch * 1024
        maps.append({
            "xa": x_prompt[c], "xb": x_sample[sb], "xqb": np.ascontiguousarray(x_sample[sb, off:off + 1024]),
            "mema": mem_prompt[c], "memb": mem_sample[sb], "wall": wall, "cols": cols, "gf": gf, "cmat": cmat,
            "r64": r64, "r128": r128,
            "r64b": np.ascontiguousarray(r64[:, :, off:off + 1024]),
            "r128b": np.ascontiguousarray(r128[:, :, off:off + 1024]),
        })
    return maps


def kernel(**inputs):
    maps = make_in_maps(**inputs)
    if "nc" not in _NC_CACHE:
        _NC_CACHE["nc"] = build_nc(FULL_CFG)
    res = run_bass_kernel_spmd(_NC_CACHE["nc"], maps, core_ids=list(range(8)))
    y_prompt = np.zeros((8, SEQ, D), np.float32)
    y_sample = np.zeros((2, SEQ, D), np.float32)
    for c in range(8):
        r = res.results[c]
        y_prompt[c] = r["ya"]
        y_sample[c // 4, (c % 4) * 1024:(c % 4 + 1) * 1024] = r["yb"]
    return (y_prompt, y_sample)
```

```python
import contextlib
import numpy as np
import concourse.bass as bass
import concourse.mybir as mybir
from concourse.bass_utils import run_bass_kernel_spmd

F32 = mybir.dt.float32
BF16 = mybir.dt.bfloat16
AF = mybir.ActivationFunctionType
ALU = mybir.AluOpType

ENGS = ("pe", "act", "dve", "pool", "sp")
EPS = 1e-6
NBUF = 12
SEQ = 4096
D = 2048

S_CKV, S_KG, S_KPE, S_VG, S_WKVB_K, S_WKVB_V = 0, 2, 4, 5, 7, 8
S_WMK, S_WMV = 9, 17
S_CQ, S_WQB = 25, 29
S_Z = [33, 73, 113]
S_WB = [41, 81, 121]
S_GL = [49, 89, 129]
S_QG, S_QM = 65, 105
S_WO = 145
NS = 161
CONV_CH = 8


class Ev:
    __slots__ = ("sem", "val")

    def __init__(self, sem, val):
        self.sem = sem
        self.val = val


class _Dummy:
    def __getitem__(self, idx):
        return self

    def __getattr__(self, name):
        return self

    def __call__(self, *a, **k):
        return self


_DUMMY = _Dummy()


def I(name, *args, **kwargs):
    return lambda eng: getattr(eng, name)(*args, **kwargs)


class T:
    def __init__(self, t, name=""):
        self.t = t
        self.name = name
        self.w = []
        self.rs = []
        self.dsem = None
        self.al = []
        self.pw_open = False
        self.psum = False
        self.pre = []
        self.pre_new = []

    def __getitem__(self, idx):
        if self.t is None:
            return _DUMMY
        return self.t[idx]


class DSem:
    def __init__(self, sem):
        self.sem = sem
        self.cnt = 0


def _compact(evs):
    best = {}
    for ev in evs:
        k = id(ev.sem)
        if k not in best or best[k].val < ev.val:
            best[k] = ev
    return list(best.values())


class B:
    def __init__(self, nc, dry=False):
        self.nc = nc
        self.dry = dry
        self.ops = {e: [] for e in ENGS}
        self.es = contextlib.ExitStack()
        self.sem = {}
        self.cnt = {e: 0 for e in ENGS}
        self.known = {e: {} for e in ENGS}
        self.final_evs = []
        self.nsem = 0
        self.maxops = 10 ** 9
        self.nops = 0
        if not dry:
            for e in ("pe", "act", "dve", "pool"):
                self.sem[e] = self.new_sem("p_" + e)

    def new_sem(self, name):
        self.nsem += 1
        return self.es.enter_context(self.nc.semaphore(name))

    def sb(self, name, shape, dtype):
        if self.dry:
            return T(None, name)
        return T(self.es.enter_context(self.nc.sbuf_tensor("s_" + name, list(shape), dtype)), name)

    def ps(self, name, shape, dtype):
        if self.dry:
            return T(None, name)
        t = T(self.es.enter_context(self.nc.psum_tensor("p_" + name, list(shape), dtype)), name)
        t.psum = True
        return t

    def dram(self, name, shape, dtype, kind=None):
        if self.dry:
            return T(None, name)
        if kind is None:
            t = self.nc.dram_tensor(name, list(shape), dtype)
        else:
            t = self.nc.dram_tensor(name, list(shape), dtype, kind=kind)
        return T(t.ap(), name)

    def _wait(self, E, ev):
        if E == "pe" and ev.sem is self.sem.get("pe"):
            return
        k = self.known[E]
        sid = id(ev.sem)
        if k.get(sid, 0) >= ev.val:
            return
        k[sid] = ev.val
        sem, val = ev.sem, ev.val
        self.ops[E].append(lambda eng: eng.wait_ge(sem, val))

    def _deps(self, E, reads, writes, pwrites=()):
        for r in reads:
            for ev in r.w:
                self._wait(E, ev)
            if r.psum:
                own = self.sem.get(E)
                for ev in r.rs:
                    if ev.sem is not own:
                        self._wait(E, ev)
            for a in r.al:
                for ev in a.w:
                    self._wait(E, ev)
        for w in writes:
            for x in [w] + w.al:
                for ev in x.w:
                    self._wait(E, ev)
                for ev in x.rs:
                    self._wait(E, ev)
        for w in pwrites:
            if w.pw_open and not w.rs:
                for ev in w.pre:
                    self._wait(E, ev)
            else:
                pre = []
                for x in [w] + w.al:
                    pre += x.w + x.rs
                pre = _compact(pre)
                for ev in pre:
                    self._wait(E, ev)
                w.pre_new = pre

    def _mark(self, ev, reads, writes, append_write=False, pwrites=()):
        for w in pwrites:
            if w.pw_open and not w.rs:
                w.w.append(ev)
                if len(w.w) > 16:
                    w.w = _compact(w.w)
            else:
                w.w = [ev]
                w.rs = []
                w.pw_open = True
                w.pre = w.pre_new
        for r in reads:
            r.rs.append(ev)
            r.pw_open = False
            if len(r.rs) > 16:
                r.rs = _compact(r.rs)
        for w in writes:
            if append_write:
                w.w.append(ev)
                if len(w.w) > 16:
                    w.w = _compact(w.w)
            else:
                w.w = [ev]
                w.rs = []
                w.pw_open = False

    def op(self, E, fns, reads=(), writes=(), pw=()):
        if self.dry:
            return None
        if callable(fns):
            fns = [fns]
        self.nops += 1
        if self.nops > self.maxops:
            return None
        self._deps(E, reads, writes, pw)
        self.cnt[E] += 1
        sem = self.sem[E]
        ev = Ev(sem, self.cnt[E])
        q = self.ops[E]
        for f in fns[:-1]:
            q.append(f)
        last = fns[-1]
        q.append(lambda eng: last(eng).then_inc(sem, 1))
        self._mark(ev, reads, writes, pwrites=pw)
        return ev

    def dma(self, Q, out_ap, in_ap, reads=(), writes=(), dsem_owner=None, append_write=False, final=False,
            after=()):
        if self.dry:
            return None
        self.nops += 1
        if self.nops > self.maxops:
            return None
        owner = dsem_owner
        if owner is None:
            owner = writes[0] if (writes and not append_write) else reads[0]
        if owner.dsem is None:
            owner.dsem = DSem(self.new_sem("d_" + owner.name))
        dsem = owner.dsem
        self._deps(Q, reads, writes)
        for ev in after:
            if ev is not None:
                self._wait(Q, ev)
        if dsem.cnt > 0:
            self._wait(Q, Ev(dsem.sem, dsem.cnt))
        dsem.cnt += 16
        ev = Ev(dsem.sem, dsem.cnt)
        sem = dsem.sem
        self.ops[Q].append(lambda eng: eng.dma_start(out=out_ap, in_=in_ap).then_inc(sem, 16))
        self._mark(ev, reads, writes, append_write=append_write)
        if final:
            self.final_evs.append(ev)
            if len(self.final_evs) > 32:
                self.final_evs = _compact(self.final_evs)
        return ev

    def finish(self):
        if self.dry:
            return
        for ev in _compact(self.final_evs):
            self._wait("sp", ev)
        nc = self.nc
        ops = self.ops
        with nc.Block() as block:
            @block.tensor
            def _(eng):
                for f in ops["pe"]:
                    f(eng)

            @block.scalar
            def _(eng):
                for f in ops["act"]:
                    f(eng)

            @block.vector
            def _(eng):
                for f in ops["dve"]:
                    f(eng)

            @block.gpsimd
            def _(eng):
                for f in ops["pool"]:
                    f(eng)

            @block.sync
            def _(eng):
                for f in ops["sp"]:
                    f(eng)
        self.es.close()


class Stream:
    def __init__(self, b, sched, resolver):
        self.b = b
        self.sched = sched
        self.rec = []
        self.resolver = resolver
        self.slots = [b.sb("ring%d" % i, [128, 2048], BF16) for i in range(NBUF)]
        self.free = list(range(NBUF))
        self.loaded = {}
        self.next_load = 0
        self.t = 0

    def _pump(self):
        while self.next_load < len(self.sched) and self.free:
            i = self.next_load
            si = self.free.pop(0)
            s = self.slots[si]
            dst, src, reads = self.resolver(self.sched[i], s)
            self.b.dma("sp", dst, src, reads=reads, writes=[s])
            self.loaded[i] = si
            self.next_load += 1

    def get(self, desc):
        if self.b.dry:
            self.rec.append(desc)
            return (len(self.rec) - 1, T(None))
        self._pump()
        i = self.t
        assert self.sched[i] == desc, (i, self.sched[i], desc)
        assert i in self.loaded, "slot pool too small: tile %d not loadable (%s)" % (i, desc)
        self.t += 1
        return (i, self.slots[self.loaded[i]])

    def release(self, h):
        if self.b.dry:
            return
        self.free.append(self.loaded.pop(h[0]))
        self._pump()


def program(b, st, cfg):
    nqb = cfg["nqb"]

    xkv = [b.dram("xa", [SEQ, D], F32, "ExternalInput"), b.dram("xb", [SEQ, D], F32, "ExternalInput")]
    xq = [xkv[0], b.dram("xqb", [1024, D], F32, "ExternalInput")]
    mem = [b.dram("mema", [256, D], F32, "ExternalInput"), b.dram("memb", [256, D], F32, "ExternalInput")]
    wall = b.dram("wall", [NS, 128, 2048], F32, "ExternalInput")
    colsd = b.dram("cols", [128, 64], F32, "ExternalInput")
    gfd = b.dram("gf", [128, D], F32, "ExternalInput")
    cmat = b.dram("cmat", [3, 128, 128], F32, "ExternalInput")
    r64 = b.dram("r64", [2, 64, SEQ], F32, "ExternalInput")
    r128 = b.dram("r128", [2, 128, SEQ], F32, "ExternalInput")
    r64b = b.dram("r64b", [2, 64, 1024], F32, "ExternalInput")
    r128b = b.dram("r128b", [2, 128, 1024], F32, "ExternalInput")
    yout = [b.dram("ya", [SEQ, D], F32, "ExternalOutput"), b.dram("yb", [1024, D], F32, "ExternalOutput")]
    wbf = b.dram("wbf", [NS, 128, 2048], BF16)
    kmla = b.dram("kmla", [2, 8, 128, SEQ], BF16)
    kped = b.dram("kped", [2, 64, SEQ], BF16)
    vmla = b.dram("vmla", [2, 4, SEQ, 256], BF16)
    kgqa = b.dram("kgqa", [2, 2, 128, SEQ], BF16)
    vgqa = b.dram("vgqa", [2, SEQ, 256], BF16)
    kmemd = b.dram("kmemd", [2, 128, 2048], BF16)
    vmemd = b.dram("vmemd", [2, 128, 2048], BF16)
    kvT = [T(None, "kvA"), T(None, "kvB")]
    conv_chunks = [(0, 2), (2, 5), (5, 9), (9, 13), (13, 17), (17, 21), (21, 25)] + [(s, s + 1) for s in range(25, NS)]
    nconv = len(conv_chunks)
    convT = [T(None, "conv%d" % i) for i in range(nconv)]
    slot2conv = {}
    for ci, (s0, s1) in enumerate(conv_chunks):
        for s in range(s0, s1):
            slot2conv[s] = ci
    convsem = T(None, "convsem")

    def resolver(desc, s):
        kind = desc[0]
        if kind == "w":
            i = desc[1]
            return s[:, :], wbf[i], [convT[slot2conv[i]]]
        if kind == "k":
            _, seq, h, half = desc
            return s[:, :], kmla[seq, h][:, half * 2048:(half + 1) * 2048], [kvT[seq]]
        if kind == "kg":
            _, seq, g, half = desc
            return s[:, :], kgqa[seq, g][:, half * 2048:(half + 1) * 2048], [kvT[seq]]
        if kind == "vp":
            _, seq, hp, j = desc
            src = vmla[seq, hp].rearrange("(kt p) c -> p kt c", p=128)[:, 8 * j:8 * j + 8, :]
            return s[:, :].rearrange("p (a c) -> p a c", c=256), src, [kvT[seq]]
        if kind == "vg":
            _, seq, j = desc
            src = vgqa[seq].rearrange("(kt p) c -> p kt c", p=128)[:, 8 * j:8 * j + 8, :]
            return s[:, :].rearrange("p (a c) -> p a c", c=256), src, [kvT[seq]]
        raise ValueError(desc)

    st.resolver = resolver

    big = b.sb("big", [128, 8192], F32)
    xt = [T(big.t, "xt0"), T(big.t, "xt1")]
    big.al = [xt[0], xt[1]]
    xt[0].al = [big]
    xt[1].al = [big]
    xn = b.sb("xn", [128, 2048], BF16)
    xn4T = [T(big.t, "xn4_%d" % i) for i in range(4)]
    for t_ in xn4T:
        t_.al = [big]
        big.al.append(t_)
    xn4 = [big[:, 4096 + i * 1024:4096 + (i + 1) * 1024].bitcast(BF16) for i in range(4)]
    hT = b.sb("hT", [128, 16, 512], BF16)
    qA = b.sb("qA", [128, 8, 512], BF16)
    qP = b.sb("qP", [128, 8, 512], BF16)
    oz = b.sb("oz", [128, 8, 512], BF16)
    cqg = b.sb("cqg", [128, 4, 512], BF16)
    sq4 = b.sb("sq4", [128, 4, 512], BF16)
    PT = [b.sb("pt%d" % i, [128, 512], BF16) for i in range(4)]
    FB = [b.sb("fb%d" % i, [128, 512], F32) for i in range(8)]
    szt = [b.sb("sz%d" % i, [128, 512], F32) for i in range(2)]
    szm = szt + [b.sb("sz%d" % i, [128, 512], F32) for i in range(2, 4)]
    rc64 = [b.sb("rc64_%d" % i, [64, 512], F32) for i in range(2)]
    rc128 = [b.sb("rc128_%d" % i, [128, 512], F32) for i in range(2)]
    kpeT = b.sb("kpeT", [128, SEQ], BF16)
    kmT = b.sb("kmT", [128, 8, 256], BF16)
    vmT = b.sb("vmT", [128, 2, 1024], BF16)
    gF = b.sb("gF", [128, D], F32)
    ident = b.sb("ident", [128, 128], BF16)
    perm64 = b.sb("perm64", [128, 128], F32)
    perm128 = b.sb("perm128", [128, 128], F32)
    ones = b.sb("ones", [128, 4, 128], BF16)
    cols = b.sb("cols", [128, 64], F32)
    epsc = b.sb("epsc", [128, 1], F32)
    onec = b.sb("onec", [128, 1], F32)
    ss = b.sb("ss", [128, 8], F32)
    vsts = [b.sb("vst%d" % i, [128, 1024], BF16) for i in range(2)]
    xsk = b.sb("xsk", [128, 512], F32)
    kgst = b.sb("kgst", [128, 2, 512], BF16)
    vgst = b.sb("vgst", [128, 4, 256], BF16)
    kpst = b.sb("kpst", [64, 512], BF16)
    P = [b.ps("ps%d" % i, [128, 512], F32) for i in range(8)]

    fbi = [0]

    def fb():
        fbi[0] += 1
        return FB[fbi[0] % len(FB)]

    roti = [0]

    def rot():
        roti[0] += 1
        return P[roti[0] % 8]

    def act(out, in_, func, reads, writes=(), pw=(), **kw):
        return b.op("act", I("activation", out=out, in_=in_, func=func, **kw), reads=reads, writes=writes, pw=pw)

    def tt(E, out, in0, in1, op, reads, writes=(), pw=()):
        return b.op(E, I("tensor_tensor", out=out, in0=in0, in1=in1, op=op), reads=reads, writes=writes, pw=pw)

    def ts(E, out, in0, s1, op0, reads, writes=(), pw=()):
        return b.op(E, I("tensor_scalar", out=out, in0=in0, scalar1=s1, scalar2=None, op0=op0),
                    reads=reads, writes=writes, pw=pw)

    def stt(out, in0, scalar, in1, op0, op1, reads, writes=(), pw=()):
        return b.op("dve", I("scalar_tensor_tensor", out=out, in0=in0, scalar=scalar, in1=in1, op0=op0, op1=op1),
                    reads=reads, writes=writes, pw=pw)

    def cp(E, out, in_, reads, writes=(), pw=()):
        if E == "act":
            return act(out, in_, AF.Copy, reads, writes, pw)
        return b.op(E, I("tensor_copy", out=out, in_=in_), reads=reads, writes=writes, pw=pw)

    def mm_group(out_ap, pairs, reads, outT):
        n = len(pairs)
        fns = [I("matmul", out_ap, lhsT=l, rhs=r, start=(i == 0), stop=(i == n - 1)) for i, (l, r) in enumerate(pairs)]
        return b.op("pe", fns, reads=reads, writes=[outT])

    b.dma("sp", cols[:, :], colsd[:, :], writes=[cols])
    b.dma("sp", gF[:, :], gfd[:, :], writes=[gF])
    b.dma("sp", perm64[:, :], cmat[1], writes=[perm64])
    b.dma("sp", perm128[:, :], cmat[2], writes=[perm128])
    b.dma("pool", ident[:, :], cmat[0], writes=[ident])
    for i, v in enumerate((1.0 / 512, 1.0 / 256, 1.0 / 128, 1.0)):
        b.op("dve", I("memset", ones[:, i, :], v), pw=[ones])
    b.op("dve", I("memset", epsc[:, :], EPS), writes=[epsc])
    b.op("dve", I("memset", qP[64:128, :, :], 0.0), writes=[qP])
    b.op("dve", I("memset", kpeT[64:128, :], 0.0), writes=[kpeT])
    b.op("dve", I("memset", onec[:, :], 1.0), writes=[onec])

    stage = cfg.get("stage", 99)
    if stage < 1:
        return
    conv_next = [0]

    def emit_conv(n=1, after=()):
        for _ in range(n):
            i = conv_next[0]
            if i >= nconv:
                return
            s0, s1 = conv_chunks[i]
            b.dma("pool", wbf[s0:s1], wall[s0:s1], writes=[convT[i]], dsem_owner=convsem, after=after)
            conv_next[0] += 1

    emit_conv(7)

    def pe_mark():
        return [] if b.dry else [Ev(b.sem["pe"], b.cnt["pe"])]

    if stage < 2:
        return

    def rstd_from(dst_ap, dstT, src_ap, srcT, n, pw=False):
        t = fb()
        act(t[:, 0:n], src_ap, AF.Ln, reads=[srcT, epsc], writes=[t], bias=epsc[:, 0:1])
        act(dst_ap, t[:, 0:n], AF.Exp, reads=[t], writes=[dstT], scale=-0.5)

    def sigmoid_from(p):
        e = fb()
        act(e[:, :], p[:, :], AF.Exp, reads=[p], writes=[e], scale=-1.0)
        act(e[:, :], e[:, :], AF.Ln, reads=[e, onec], writes=[e], bias=onec[:, 0:1])
        s = fb()
        act(s[:, :], e[:, :], AF.Exp, reads=[e], writes=[s], scale=-1.0)
        return s

    def norm_a(src_ap, i, xnT, xn_ap):
        x = xt[i % 2]
        xo = (i % 2) * 2048
        xap = x[:, xo:xo + 2048]
        b.dma("sp", xap, src_ap, writes=[x])
        act(xn_ap, xap, AF.Square, reads=[x], writes=[xnT, ss], scale=float(1.0 / np.sqrt(2048.0)),
            accum_out=ss[:, 0:1])
        rstd_from(ss[:, 1:2], ss, ss[:, 0:1], ss, 1)
        ts("dve", xn_ap, xap, ss[:, 1:2], ALU.mult, reads=[x, ss], writes=[xnT])

    def norm_b(xnT, xn_ap, gcol0, tk):
        for half in range(2):
            pb = rot()
            pbv = pb[:, :].bitcast(BF16)
            fns = [I("transpose", out=pbv[:, j * 128:(j + 1) * 128],
                     in_=xn_ap[:, (half * 8 + j) * 128:(half * 8 + j + 1) * 128], identity=ident[:, :]) for j in range(8)]
            b.op("pe", fns, reads=[xnT, ident], writes=[pb])
            for j in range(8):
                kc = half * 8 + j
                dst = hT[:, kc, tk * 128:(tk + 1) * 128]
                src = pbv[:, j * 128:(j + 1) * 128]
                g = cols[:, gcol0 + kc:gcol0 + kc + 1]
                if half == 0:
                    act(dst, src, AF.Copy, reads=[pb, cols], pw=[hT], scale=g)
                else:
                    ts("dve", dst, src, g, ALU.mult, reads=[pb, cols], pw=[hT])

    def load_x_and_norm(src_ap, gcol0, tk, i):
        norm_a(src_ap, i, xn, xn[:, :])
        norm_b(xn, xn[:, :], gcol0, tk)

    def proj_lhsT(slot_desc, ncols=128, nk=16, ntok=512):
        h = st.get(slot_desc)
        s = h[1]
        p = rot()
        pairs = [(s[:, kc * ncols:(kc + 1) * ncols], hT[:, kc, 0:ntok]) for kc in range(nk)]
        mm_group(p[0:ncols, 0:ntok], pairs, [s, hT], p)
        st.release(h)
        return p

    def load_rope(seq, is_q, t0):
        if is_q and seq == 1:
            a64, a128 = r64b, r128b
        else:
            a64, a128 = r64, r128
        for i in range(2):
            b.dma("sp", rc64[i][:, :], a64[i][:, t0:t0 + 512], writes=[rc64[i]])
            b.dma("sp", rc128[i][:, :], a128[i][:, t0:t0 + 512], writes=[rc128[i]])

    add_eng = ["dve"]

    def rope(xs, n, dst_ap, dstT, scale_ap=None, scaleT=None):
        perm = perm64 if n == 64 else perm128
        rc = rc64 if n == 64 else rc128
        pr = rot()
        mm_group(pr[0:n, :], [(perm[0:n, 0:n], xs[0:n, :])], [perm, xs], pr)
        t1 = fb()
        tt("dve", t1[0:n, :], xs[0:n, :], rc[0][0:n, :], ALU.mult, reads=[xs, rc[0]], writes=[t1])
        t2 = fb()
        tt("dve", t2[0:n, :], pr[0:n, :], rc[1][0:n, :], ALU.mult, reads=[pr, rc[1]], writes=[t2])
        if scale_ap is None:
            tt(add_eng[0], dst_ap, t1[0:n, :], t2[0:n, :], ALU.add, reads=[t1, t2], pw=[dstT])
        else:
            tt(add_eng[0], t1[0:n, :], t1[0:n, :], t2[0:n, :], ALU.add, reads=[t1, t2], writes=[t1])
            tt("dve", dst_ap, t1[0:n, :], scale_ap, ALU.mult, reads=[t1, scaleT], pw=[dstT])

    def hnr_a(p, gcol, xg, sq):
        act(xg[:, :], p[:, :], AF.Copy, reads=[p, cols], writes=[xg], scale=cols[:, gcol:gcol + 1])
        act(sq[:, :], p[:, :], AF.Square, reads=[p], writes=[sq])

    def hnr_b(xg, sq, dst_ap, dstT):
        pm = rot()
        mm_group(pm[:, :], [(ones[:, 2, :], sq[:, :])], [ones, sq], pm)
        r = fb()
        rstd_from(r[:, :], r, pm[:, :], pm, 512)
        rope(xg, 128, dst_ap, dstT, scale_ap=r[:, :], scaleT=r)

    def head_norm_rope(p, gcol, dst_ap, dstT):
        xg = fb()
        hnr_a(p, gcol, xg, PT[3])
        hnr_b(xg, PT[3], dst_ap, dstT)

    def kv_mem(seq):
        kv = kvT[seq]
        for mt in range(2):
            load_x_and_norm(mem[seq][mt * 128:(mt + 1) * 128, :], 16, mt, mt)
        kmst = oz
        for c in range(8):
            p = proj_lhsT(("w", S_WMK + c), ntok=256)
            cp("act" if c % 2 == 0 else "dve", kmst[:, c, 0:256], p[:, 0:256], reads=[p], pw=[kmst])
        vmst = cqg
        for cb in range(2):
            hs = [st.get(("w", S_WMV + cb * 4 + g)) for g in range(4)]
            for mt in range(2):
                p = rot()
                pairs = [(hT[:, kc, mt * 128:(mt + 1) * 128],
                          hs[kc // 4][1][:, (kc % 4) * 512:(kc % 4 + 1) * 512]) for kc in range(16)]
                mm_group(p[:, :], pairs, [hT] + [h[1] for h in hs], p)
                cp("dve" if mt == 0 else "act", vmst[:, 2 * mt + cb, :], p[:, :], reads=[p], pw=[vmst])
            for h in hs:
                st.release(h)
        b.dma("sp", kmemd[seq].rearrange("p (c m) -> p c m", m=256), kmst[:, :, 0:256],
              reads=[kmst], writes=[kv], append_write=True)
        b.dma("sp", vmemd[seq].rearrange("p (a c) -> p a c", c=512), vmst[:, :, :],
              reads=[vmst], writes=[kv], append_write=True)

    def kv_pass(seq):
        kv = kvT[seq]
        ng = cfg["nkvgrp"]
        ckvn = cqg
        kst = qA
        xgk = szt
        sqk = [PT[2], PT[3]]

        def prep_a(grp, tk):
            t0 = grp * 512
            norm_a(xkv[seq][t0 + tk * 128:t0 + (tk + 1) * 128, :], tk, xn4T[tk], xn4[tk])

        def prep_b(grp, tk):
            norm_b(xn4T[tk], xn4[tk], 0, tk)

        def stage1(grp):
            t0 = grp * 512
            load_rope(seq, False, t0)
            for c in range(2):
                p = proj_lhsT(("w", S_CKV + c))
                act(ckvn[:, c, :], p[:, :], AF.Copy, reads=[p, cols], pw=[ckvn], scale=cols[:, 36 + c:37 + c])
                act(sq4[:, c, :], p[:, :], AF.Square, reads=[p], pw=[sq4])
            for g in range(2):
                p = proj_lhsT(("w", S_KG + g))
                hnr_a(p, 39, xgk[g], sqk[g])
            p = proj_lhsT(("w", S_KPE), ncols=64)
            act(xsk[0:64, :], p[0:64, :], AF.Copy, reads=[p], writes=[xsk])
            hs = [st.get(("w", S_VG + i)) for i in range(2)]
            for tk in range(4):
                p = rot()
                pairs = [(hT[:, kc, tk * 128:(tk + 1) * 128],
                          hs[kc // 8][1][:, (kc % 8) * 256:(kc % 8 + 1) * 256]) for kc in range(16)]
                mm_group(p[:, 0:256], pairs, [hT] + [h[1] for h in hs], p)
                cp("dve", vgst[:, tk, :], p[:, 0:256], reads=[p], pw=[vgst])
            for h in hs:
                st.release(h)
            b.dma("sp", vgqa[seq][t0:t0 + 512, :].rearrange("(a p) c -> p a c", p=128), vgst[:, :, :],
                  reads=[vgst], writes=[kv], append_write=True)
            emit_conv(3, after=pe_mark())

        def stage2_parts(grp):
            t0 = grp * 512
            state = {}

            def kexp(h0, h1):
                hk = state["hk"]
                for h in range(h0, h1):
                    p = rot()
                    pairs = [(hk[1][:, fc * 1024 + h * 128:fc * 1024 + (h + 1) * 128], ckvn[:, fc, :]) for fc in range(2)]
                    mm_group(p[:, :], pairs, [hk[1], ckvn], p)
                    cp("act" if h % 2 == 0 else "dve", kst[:, h, :], p[:, :], reads=[p], pw=[kst])

            def vexp(tk):
                hv = state["hv"]
                vst = vsts[tk % 2]
                for cb in range(2):
                    p = rot()
                    pairs = [(ckvn[:, fc, tk * 128:(tk + 1) * 128],
                              hv[1][:, fc * 1024 + cb * 512:fc * 1024 + (cb + 1) * 512]) for fc in range(2)]
                    mm_group(p[:, :], pairs, [hv[1], ckvn], p)
                    cp("act" if cb == 0 else "dve", vst[:, cb * 512:(cb + 1) * 512], p[:, :], reads=[p], pw=[vst])
                tkk = t0 + tk * 128
                b.dma("sp", vmla[seq].rearrange("hp t c -> t hp c")[tkk:tkk + 128],
                      vst[:, :].rearrange("p (a c) -> p a c", c=256),
                      reads=[vst], writes=[kv], append_write=True)

            def part0():
                pm = rot()
                mm_group(pm[:, :], [(ones[:, 1, :], sq4[:, c, :]) for c in range(2)], [ones, sq4], pm)
                r = fb()
                rstd_from(r[:, :], r, pm[:, :], pm, 512)
                for c in range(2):
                    tt("dve", ckvn[:, c, :], ckvn[:, c, :], r[:, :], ALU.mult, reads=[ckvn, r], writes=[ckvn])
                rope(xsk, 64, kpst[:, :], kpst)
                b.dma("sp", kped[seq][:, t0:t0 + 512], kpst[:, :], reads=[kpst], writes=[kv], append_write=True)
                state["hk"] = st.get(("w", S_WKVB_K))
                kexp(0, 4)

            def part1():
                kexp(4, 8)
                st.release(state["hk"])
                b.dma("sp", kmla[seq].rearrange("h d t -> d h t")[:, :, t0:t0 + 512], kst[:, :, :],
                      reads=[kst], writes=[kv], append_write=True)
                hnr_b(xgk[0], sqk[0], kgst[:, 0, :], kgst)

            def part2():
                state["hv"] = st.get(("w", S_WKVB_V))
                vexp(0)
                vexp(1)
                hnr_b(xgk[1], sqk[1], kgst[:, 1, :], kgst)
                b.dma("sp", kgqa[seq].rearrange("g d t -> d g t")[:, :, t0:t0 + 512], kgst[:, :, :],
                      reads=[kgst], writes=[kv], append_write=True)

            def part3():
                vexp(2)
                vexp(3)
                st.release(state["hv"])

            return [part0, part1, part2, part3]

        for tk in range(4):
            prep_a(0, tk)
        for tk in range(4):
            prep_b(0, tk)
        for grp in range(ng):
            stage1(grp)
            parts = stage2_parts(grp)
            if grp + 1 < ng:
                for tk in range(4):
                    prep_a(grp + 1, tk)
            for k in range(4):
                parts[k]()
                emit_conv(2 if k % 2 == 0 else 1, after=pe_mark())
                if grp + 1 < ng:
                    prep_b(grp + 1, k)
        if cfg.get("stage", 99) >= 3:
            kv_mem(seq)

    def attention(nkt, qk_pairs, qk_reads, v_lhsT, v_reads, scale, Ob, Sb):
        def qk(kt):
            S = P[kt % 3]
            mm_group(S[:, :], qk_pairs(kt), qk_reads(kt), S)

        def ex(kt):
            S = P[kt % 3]
            pt = PT[kt % 3]
            act(pt[:, :], S[:, :], AF.Exp, reads=[S], writes=[pt], scale=float(scale))

        def pv(kt):
            pt = PT[kt % 3]
            fns = [I("matmul", Ob[:, :], lhsT=v_lhsT(kt), rhs=pt[:, :], start=(kt == 0), stop=(kt == nkt - 1)),
                   I("matmul", Sb[:, :], lhsT=ones[:, 3, :], rhs=pt[:, :], start=(kt == 0), stop=(kt == nkt - 1))]
            b.op("pe", fns, reads=[pt, ones] + v_reads(kt), writes=[Ob, Sb])

        qk(0)
        if nkt > 1:
            qk(1)
        for kt in range(nkt):
            ex(kt)
            if kt + 2 < nkt:
                qk(kt + 2)
            pv(kt)

    def finish_head(Ob, Sb, sz, dst_ap):
        t = fb()
        act(t[:, :], Sb[:, :], AF.Ln, reads=[Sb], writes=[t])
        rec = fb()
        act(rec[:, :], t[:, :], AF.Exp, reads=[t], writes=[rec], scale=-1.0)
        a = fb()
        tt("dve", a[:, :], Ob[:, :], rec[:, :], ALU.mult, reads=[Ob, rec], writes=[a])
        tt("dve", dst_ap, a[:, :], sz[:, :], ALU.mult, reads=[a, sz], pw=[oz])

    def zproj(i, c, sz):
        h = st.get(("w", S_Z[i] + c))
        s = h[1]
        p = P[7]
        mm_group(p[:, :], [(s[:, kc * 128:(kc + 1) * 128], hT[:, kc, :]) for kc in range(16)], [s, hT], p)
        st.release(h)
        sg = sigmoid_from(p)
        tt("dve", sz[:, :], sg[:, :], p[:, :], ALU.mult, reads=[sg, p], writes=[sz])

    def branch_proj(i):
        hb = None
        for cc in range(16):
            if cc % 2 == 0:
                hb = st.get(("w", S_WB[i] + cc // 2))
            sb_ = hb[1]
            pt_ = rot()
            o0 = (cc % 2) * 1024
            mm_group(pt_[:, :], [(sb_[:, o0 + kc * 128:o0 + (kc + 1) * 128], oz[:, kc, :]) for kc in range(8)],
                     [sb_, oz], pt_)
            if cc % 2 == 1:
                st.release(hb)
            pg = proj_lhsT(("w", S_GL[i] + cc))
            sg = sigmoid_from(pg)
            mo = cc * 512
            if i == 0:
                tt("dve", big[:, mo:mo + 512], sg[:, :], pt_[:, :], ALU.mult, reads=[sg, pt_], pw=[big])
            else:
                tm = fb()
                tt("dve", tm[:, :], sg[:, :], pt_[:, :], ALU.mult, reads=[sg, pt_], writes=[tm])
                tt("pool", big[:, mo:mo + 512], big[:, mo:mo + 512], tm[:, :], ALU.add, reads=[tm, big], writes=[big])

    def q_block(seq, qb):
        kv = kvT[seq]
        t0 = qb * 512
        if qb == 0:
            b.dma("sp", kpeT[0:64, :], kped[seq], reads=[kv], writes=[kpeT])
            b.dma("sp", kmT[:, :, :], kmemd[seq].rearrange("p (c m) -> p c m", m=256), reads=[kv], writes=[kmT])
            b.dma("sp", vmT[:, :, :], vmemd[seq].rearrange("p (a c) -> p a c", c=1024), reads=[kv], writes=[vmT])
        load_rope(seq, True, t0)
        for tk in range(4):
            load_x_and_norm(xq[seq][t0 + tk * 128:t0 + (tk + 1) * 128, :], 0, tk, tk)

        cqn = cqg
        for c in range(4):
            p = proj_lhsT(("w", S_CQ + c))
            act(cqn[:, c, :], p[:, :], AF.Copy, reads=[p, cols], pw=[cqn], scale=cols[:, 32 + c:33 + c])
            act(sq4[:, c, :], p[:, :], AF.Square, reads=[p], pw=[sq4])
        pm = rot()
        mm_group(pm[:, :], [(ones[:, 0, :], sq4[:, c, :]) for c in range(4)], [ones, sq4], pm)
        r = fb()
        rstd_from(r[:, :], r, pm[:, :], pm, 512)
        for c in range(4):
            tt("dve", cqn[:, c, :], cqn[:, c, :], r[:, :], ALU.mult, reads=[cqn, r], writes=[cqn])
        hqs = {}

        def q_proj(h):
            if h % 2 == 0:
                hqs[0] = st.get(("w", S_WQB + h // 2))
            s = hqs[0][1]
            base = (h % 2) * 768
            p = rot()
            mm_group(p[:, :], [(s[:, base + fc * 128:base + (fc + 1) * 128], cqn[:, fc, :]) for fc in range(4)],
                     [s, cqn], p)
            cp("act", qA[:, h, :], p[:, :], reads=[p], pw=[qA])
            p2 = rot()
            mm_group(p2[0:64, :], [(s[:, base + 512 + fc * 64:base + 512 + (fc + 1) * 64], cqn[:, fc, :])
                                   for fc in range(4)], [s, cqn], p2)
            if h % 2 == 1:
                st.release(hqs[0])
            xs = fb()
            act(xs[0:64, :], p2[0:64, :], AF.Copy, reads=[p2], writes=[xs])
            return xs

        xs_prev = q_proj(0)
        for h in range(1, 8):
            xs_cur = q_proj(h)
            rope(xs_prev, 64, qP[0:64, h - 1, :], qP)
            xs_prev = xs_cur
        rope(xs_prev, 64, qP[0:64, 7, :], qP)
        sc = 1.0 / np.sqrt(192.0)
        vps = None
        for h in range(8):
            sz = szt[h % 2]
            zproj(0, h, sz)
            k0 = st.get(("k", seq, h, 0))
            if h % 2 == 0:
                vps = [st.get(("vp", seq, h // 2, j)) for j in range(4)]
            k1 = st.get(("k", seq, h, 1))
            ks = [k0, k1]
            Ob, Sb = P[3 + h % 2], P[5 + h % 2]
            attention(
                32,
                lambda kt: [(ks[kt // 16][1][:, (kt % 16) * 128:(kt % 16 + 1) * 128], qA[:, h, :]),
                            (kpeT[:, kt * 128:(kt + 1) * 128], qP[:, h, :])],
                lambda kt: [ks[kt // 16][1], qA, kpeT, qP],
                lambda kt: vps[kt // 8][1][:, (kt % 8) * 256 + (h % 2) * 128:(kt % 8) * 256 + (h % 2 + 1) * 128],
                lambda kt: [vps[kt // 8][1]],
                sc, Ob, Sb)
            st.release(k0)
            st.release(k1)
            if h % 2 == 1:
                for v in vps:
                    st.release(v)
            finish_head(Ob, Sb, sz, oz[:, h, :])
        branch_proj(0)

        def g_a(h):
            p = proj_lhsT(("w", S_QG + h))
            hnr_a(p, 38, szt[h % 2], PT[2 + h % 2])

        g_a(0)
        for h in range(1, 8):
            g_a(h)
            hnr_b(szt[(h - 1) % 2], PT[2 + (h - 1) % 2], qA[:, h - 1, :], qA)
        hnr_b(szt[1], PT[3], qA[:, 7, :], qA)
        sc = 1.0 / np.sqrt(128.0)
        vgs = None
        ks = None
        for g in range(2):
            for hh in range(4):
                h = g * 4 + hh
                sz = szt[h % 2]
                zproj(1, h, sz)
                if hh == 0:
                    k0 = st.get(("kg", seq, g, 0))
                    if g == 0:
                        vgs = [st.get(("vg", seq, j)) for j in range(4)]
                    k1 = st.get(("kg", seq, g, 1))
                    ks = [k0, k1]
                Ob, Sb = P[3 + h % 2], P[5 + h % 2]
                attention(
                    32,
                    lambda kt: [(ks[kt // 16][1][:, (kt % 16) * 128:(kt % 16 + 1) * 128], qA[:, h, :])],
                    lambda kt: [ks[kt // 16][1], qA],
                    lambda kt: vgs[kt // 8][1][:, (kt % 8) * 256 + g * 128:(kt % 8) * 256 + (g + 1) * 128],
                    lambda kt: [vgs[kt // 8][1]],
                    sc, Ob, Sb)
                finish_head(Ob, Sb, sz, oz[:, h, :])
            st.release(ks[0])
            st.release(ks[1])
        for v in vgs:
            st.release(v)
        branch_proj(1)

        for c in range(8):
            p = proj_lhsT(("w", S_QM + c))
            cp("act" if c % 2 == 0 else "dve", qA[:, c, :], p[:, :], reads=[p], pw=[qA])
        sc = 1.0 / np.sqrt(256.0)
        SM = [[P[0], P[1]], [P[2], P[5]]]

        def m_a(h):
            for dvc in range(2):
                zproj(2, 2 * h + dvc, szm[(h % 2) * 2 + dvc])
            for mt in range(2):
                S = SM[h % 2][mt]
                mm_group(S[:, :], [(kmT[:, 2 * h + dc, mt * 128:(mt + 1) * 128], qA[:, 2 * h + dc, :])
                                   for dc in range(2)], [kmT, qA], S)
                pt = PT[(h % 2) * 2 + mt]
                act(pt[:, :], S[:, :], AF.Exp, reads=[S], writes=[pt], scale=float(sc))

        def m_b(h):
            pts = [PT[(h % 2) * 2 + mt] for mt in range(2)]
            Sb = P[6]
            mm_group(Sb[:, :], [(ones[:, 3, :], pts[mt][:, :]) for mt in range(2)], [ones] + pts, Sb)
            for dvc in range(2):
                Ob = P[3 + dvc]
                mm_group(Ob[:, :], [(vmT[:, mt, h * 256 + dvc * 128:h * 256 + (dvc + 1) * 128], pts[mt][:, :])
                                    for mt in range(2)], [vmT] + pts, Ob)
                finish_head(Ob, Sb, szm[(h % 2) * 2 + dvc], oz[:, 2 * h + dvc, :])

        m_a(0)
        for h in range(1, 4):
            m_a(h)
            m_b(h - 1)
        m_b(3)
        branch_proj(2)

        mbf = hT
        for cc in range(16):
            cp("dve" if cc % 2 == 0 else "act", mbf[:, cc, :], big[:, cc * 512:(cc + 1) * 512], reads=[big], pw=[mbf])

        for pair in range(2):
            for j in range(2):
                tk = pair * 2 + j
                x = xt[j]
                b.dma("sp", x[:, j * 2048:(j + 1) * 2048], xq[seq][t0 + tk * 128:t0 + (tk + 1) * 128, :], writes=[x])
            for cb in range(4):
                hs = [st.get(("w", S_WO + cb * 4 + g)) for g in range(4)]
                for j in range(2):
                    tk = pair * 2 + j
                    x = xt[j]
                    p = rot()
                    pairs = [(mbf[:, kc, tk * 128:(tk + 1) * 128],
                              hs[kc // 4][1][:, (kc % 4) * 512:(kc % 4 + 1) * 512]) for kc in range(16)]
                    mm_group(p[:, :], pairs, [mbf] + [h[1] for h in hs], p)
                    xa = x[:, j * 2048 + cb * 512:j * 2048 + (cb + 1) * 512]
                    tt("dve", xa, p[:, :], xa, ALU.add, reads=[p, x], writes=[x])
                for h in hs:
                    st.release(h)
            for j in range(2):
                tk = pair * 2 + j
                x = xt[j]
                xa = x[:, j * 2048:(j + 1) * 2048]
                c0 = 2 + 2 * j
                act(xn[:, :], xa, AF.Square, reads=[x], writes=[xn, ss], scale=float(1.0 / np.sqrt(2048.0)),
                    accum_out=ss[:, c0:c0 + 1])
                rstd_from(ss[:, c0 + 1:c0 + 2], ss, ss[:, c0:c0 + 1], ss, 1)
                stt(xa, xa, ss[:, c0 + 1:c0 + 2], gF[:, :], ALU.mult, ALU.mult, reads=[x, ss, gF], writes=[x])
                b.dma("sp", yout[seq][t0 + tk * 128:t0 + (tk + 1) * 128, :], xa, reads=[x], final=True,
                      dsem_owner=x)

    for seq in cfg["kvseqs"]:
        kv_pass(seq)
    emit_conv(nconv)
    add_eng[0] = "pool"
    for seq in range(2):
        for qb in range(nqb[seq]):
            q_block(seq, qb)


FULL_CFG = {"nqb": [8, 2], "kvseqs": [0, 1], "nkvgrp": 8}
_NC_CACHE = {}


def build_nc(cfg=FULL_CFG):
    bd = B(None, dry=True)
    std = Stream(bd, None, None)
    program(bd, std, cfg)
    sched = std.rec
    nc = bass.Bass("TRN2", target_bir_lowering=False)
    b = B(nc)
    b.maxops = cfg.get("maxops", 10 ** 9)
    st = Stream(b, sched, None)
    program(b, st, cfg)
    assert st.t == len(sched), (st.t, len(sched))
    print("nops", b.nops, "nsem", b.nsem)
    b.finish()
    return nc


def _lhsT_tile(W, c0, ncols):
    K = W.shape[0]
    nk = K // 128
    t = W[:, c0:c0 + ncols].reshape(nk, 128, ncols).transpose(1, 0, 2).reshape(128, nk * ncols)
    return t


def _rhs_tile(W, c0, ncols, k0, nk):
    t = W[:, c0:c0 + ncols].reshape(-1, 128, ncols)[k0:k0 + nk].transpose(1, 0, 2).reshape(128, nk * ncols)
    return t


def _build_wall(w_in, w_q_b, w_kv_b, w_mem_kv, w_branch, w_out):
    wall = np.zeros((NS, 128, 2048), np.float32)

    def put(s, t):
        wall[s, :, :t.shape[1]] = t

    CQ0, CKV0, KPE0, QG0, KG0, VG0, QM0, Z0, GL0 = 0, 512, 768, 832, 1856, 2112, 2368, 3392, 6464
    for c in range(2):
        put(S_CKV + c, _lhsT_tile(w_in, CKV0 + c * 128, 128))
        put(S_KG + c, _lhsT_tile(w_in, KG0 + c * 128, 128))
    put(S_KPE, _lhsT_tile(w_in, KPE0, 64))
    for i in range(2):
        put(S_VG + i, _rhs_tile(w_in, VG0, 256, 8 * i, 8))
    kvb = w_kv_b.reshape(2, 128, 8, 256)
    put(S_WKVB_K, kvb[:, :, :, :128].transpose(1, 0, 2, 3).reshape(128, 2048))
    put(S_WKVB_V, kvb[:, :, :, 128:].transpose(1, 0, 2, 3).reshape(128, 2048))
    for c in range(8):
        put(S_WMK + c, _lhsT_tile(w_mem_kv, c * 128, 128))
    for cb in range(2):
        for g in range(4):
            put(S_WMV + cb * 4 + g, _rhs_tile(w_mem_kv, 1024 + cb * 512, 512, 4 * g, 4))
    for c in range(4):
        put(S_CQ + c, _lhsT_tile(w_in, CQ0 + c * 128, 128))
    for j in range(4):
        for hl in range(2):
            h = 2 * j + hl
            wall[S_WQB + j, :, hl * 768:hl * 768 + 512] = _lhsT_tile(w_q_b, h * 192, 128)
            wall[S_WQB + j, :, hl * 768 + 512:hl * 768 + 768] = _lhsT_tile(w_q_b, h * 192 + 128, 64)
    for i in range(3):
        for c in range(8):
            put(S_Z[i] + c, _lhsT_tile(w_in, Z0 + i * 1024 + c * 128, 128))
            t = np.concatenate([_lhsT_tile(w_branch[i], (2 * c + half) * 128, 128) for half in range(2)], axis=1)
            put(S_WB[i] + c, t)
        for cc in range(16):
            put(S_GL[i] + cc, _lhsT_tile(w_in, GL0 + i * 2048 + cc * 128, 128))
    for h in range(8):
        put(S_QG + h, _lhsT_tile(w_in, QG0 + h * 128, 128))
        put(S_QM + h, _lhsT_tile(w_in, QM0 + h * 128, 128))
    for cb in range(4):
        for g in range(4):
            put(S_WO + cb * 4 + g, _rhs_tile(w_out, cb * 512, 512, 4 * g, 4))
    return wall


def _perm(n, half):
    m = np.zeros((128, 128), np.float32)
    blk = 2 * half
    for j in range(n):
        if j % blk < half:
            m[j + half, j] = -1.0
        else:
            m[j - half, j] = 1.0
    return m


def _rope_table(n, half):
    t = np.arange(SEQ)
    row = (t // 64).astype(np.float32)
    col = (t % 64).astype(np.float32)
    freqs = (np.float32(10000.0) ** (-(np.arange(half, dtype=np.float32) / np.float32(half)))).astype(np.float32)
    out = np.zeros((2, n, SEQ), np.float32)
    for p in range(n):
        pos = row if p < n // 2 else col
        ang = (pos * freqs[p % half]).astype(np.float32)
        out[0, p] = np.cos(ang)
        out[1, p] = np.sin(ang)
    return out


def make_in_maps(x_prompt, x_sample, mem_prompt, mem_sample, norm_g, w_in, mla_q_norm_g, w_q_b,
                 mla_kv_norm_g, w_kv_b, gqa_q_norm_g, gqa_k_norm_g, mem_norm_g, w_mem_kv,
                 w_branch, w_out, final_norm_g, cores=range(8)):
    f = lambda a: np.ascontiguousarray(np.asarray(a, dtype=np.float32))
    wall = _build_wall(f(w_in)[0], f(w_q_b)[0], f(w_kv_b)[0], f(w_mem_kv)[0], f(w_branch)[0], f(w_out)[0])
    cols = np.zeros((128, 64), np.float32)
    cols[:, 0:16] = f(norm_g)[0].reshape(16, 128).T
    cols[:, 16:32] = f(mem_norm_g)[0].reshape(16, 128).T
    cols[:, 32:36] = f(mla_q_norm_g)[0].reshape(4, 128).T
    cols[:, 36:38] = f(mla_kv_norm_g)[0].reshape(2, 128).T
    cols[:, 38] = f(gqa_q_norm_g)[0]
    cols[:, 39] = f(gqa_k_norm_g)[0]
    gf = np.ascontiguousarray(np.broadcast_to(f(final_norm_g)[None, :], (128, D)))
    cmat = np.stack([np.eye(128, dtype=np.float32), _perm(64, 16), _perm(128, 32)])
    r64 = _rope_table(64, 16)
    r128 = _rope_table(128, 32)
    x_prompt, x_sample, mem_prompt, mem_sample = f(x_prompt), f(x_sample), f(mem_prompt), f(mem_sample)
    maps = []
    for c in cores:
        sb, ch = c // 4, c % 4
        off = ch * 1024
        maps.append({
            "xa": x_prompt[c], "xb": x_sample[sb], "xqb": np.ascontiguousarray(x_sample[sb, off:off + 1024]),
            "mema": mem_prompt[c], "memb": mem_sample[sb], "wall": wall, "cols": cols, "gf": gf, "cmat": cmat,
            "r64": r64, "r128": r128,
            "r64b": np.ascontiguousarray(r64[:, :, off:off + 1024]),
            "r128b": np.ascontiguousarray(r128[:, :, off:off + 1024]),
        })
    return maps


def kernel(**inputs):
    maps = make_in_maps(**inputs)
    if "nc" not in _NC_CACHE:
        _NC_CACHE["nc"] = build_nc(FULL_CFG)
    res = run_bass_kernel_spmd(_NC_CACHE["nc"], maps, core_ids=list(range(8)))
    y_prompt = np.zeros((8, SEQ, D), np.float32)
    y_sample = np.zeros((2, SEQ, D), np.float32)
    for c in range(8):
        r = res.results[c]
        y_prompt[c] = r["ya"]
        y_sample[c // 4, (c % 4) * 1024:(c % 4 + 1) * 1024] = r["yb"]
    return (y_prompt, y_sample)
```
